# Optimizing a Trainium2 kernel written in Bass

```python
import math
import jax, jax.numpy as jnp
from jax import lax
import numpy as np

D_MODEL = 1024
BATCH = 8
SEQ = 2048
DEPTH = 1
DEC_BATCH = 4
DEC_SEQ = 8192
PAST_LEN = 128

MLA_HEADS = 16
QK_NOPE = 64
QK_ROPE = 32
V_DIM = 64
Q_LORA = 384
KV_LORA = 256
ROPE_THETA = 10000.0
DIL_CONFIGS = ((128, 1), (512, 4), (2048, 16))
DIL_HEADS_PER_GROUP = 4
DIL_HEADS = DIL_HEADS_PER_GROUP * len(DIL_CONFIGS)
DIL_HEAD_DIM = 64
D_FF = 2816
CONV_WIDTH = 3
Q_BLOCK = 128
EPS = 1e-6
IN_SPLITS = (Q_LORA, KV_LORA, QK_ROPE, 3 * DIL_HEADS * DIL_HEAD_DIM, D_MODEL, D_MODEL)
IN_COLS = Q_LORA + KV_LORA + QK_ROPE + 3 * DIL_HEADS * DIL_HEAD_DIM + 2 * D_MODEL

kernel_name = 'hybrid_mla_dilated_convffn_adaln_encoder'


def rmsnorm(x, g):
    xf = x.astype(jnp.float32)
    y = xf * lax.rsqrt(jnp.mean(xf * xf, axis=-1, keepdims=True) + EPS)
    return (y * g.astype(jnp.float32)).astype(x.dtype)


def rope_cos_sin(seq_len):
    inv = ROPE_THETA ** (-jnp.arange(0, QK_ROPE, 2, dtype=jnp.float32) / QK_ROPE)
    ang = jnp.arange(seq_len, dtype=jnp.float32)[:, None] * inv[None, :]
    return jnp.cos(ang), jnp.sin(ang)


def apply_rope(x, cos, sin):
    cos = cos.astype(x.dtype)
    sin = sin.astype(x.dtype)
    x1, x2 = jnp.split(x, 2, axis=-1)
    return jnp.concatenate([x1 * cos - x2 * sin, x1 * sin + x2 * cos], axis=-1)


def mla_attention(q_nope, q_rope, k_nope, k_rope, v):
    B, S, H, _ = q_nope.shape
    nb = S // Q_BLOCK
    scale = (QK_NOPE + QK_ROPE) ** -0.5
    qn = q_nope.reshape(B, nb, Q_BLOCK, H, QK_NOPE).transpose(1, 0, 2, 3, 4)
    qr = q_rope.reshape(B, nb, Q_BLOCK, H, QK_ROPE).transpose(1, 0, 2, 3, 4)

    def block(args):
        qn_b, qr_b = args
        s = (jnp.einsum('bqhd,bkhd->bhqk', qn_b, k_nope)
             + jnp.einsum('bqhr,bkr->bhqk', qr_b, k_rope)).astype(jnp.float32) * scale
        p = jax.nn.softmax(s, axis=-1)
        return jnp.einsum('bhqk,bkhd->bqhd', p.astype(v.dtype), v)

    o = lax.map(block, (qn, qr))
    return o.transpose(1, 0, 2, 3, 4).reshape(B, S, H * V_DIM)


def dilated_group_attention(q, k, v, slopes, window, dilation):
    B, S, Hg, dh = q.shape
    n_side = (window // 2) // dilation
    pad = n_side * dilation
    offsets = jnp.arange(-n_side, n_side + 1) * dilation
    kp = jnp.pad(k, ((0, 0), (pad, pad), (0, 0), (0, 0)))
    vp = jnp.pad(v, ((0, 0), (pad, pad), (0, 0), (0, 0)))
    alibi = -slopes[:, None] * jnp.abs(offsets).astype(jnp.float32)[None, :]
    scale = dh ** -0.5
    nb = S // Q_BLOCK
    qb = q.reshape(B, nb, Q_BLOCK, Hg, dh).transpose(1, 0, 2, 3, 4)
    starts = jnp.arange(nb) * Q_BLOCK

    def block(args):
        q_b, t0 = args
        pos = t0 + jnp.arange(Q_BLOCK)[:, None] + offsets[None, :]
        valid = (pos >= 0) & (pos < S)
        idx = pos + pad
        k_g = jnp.take(kp, idx, axis=1)
        v_g = jnp.take(vp, idx, axis=1)
        s = jnp.einsum('bqhd,bqjhd->bhqj', q_b, k_g).astype(jnp.float32) * scale + alibi[None, :, None, :]
        s = jnp.where(valid[None, None], s, -jnp.inf)
        lse = jax.nn.logsumexp(s, axis=-1)
        p = jnp.exp(s - lse[..., None])
        o = jnp.einsum('bhqj,bqjhd->bqhd', p.astype(v.dtype), v_g)
        return o, lse.transpose(0, 2, 1)

    o, lse = lax.map(block, (qb, starts))
    o = o.transpose(1, 0, 2, 3, 4).reshape(B, S, Hg, dh)
    lse = lse.transpose(1, 0, 2, 3).reshape(B, S, Hg)
    return o, lse


def dwconv_centered(u, w, b):
    S = u.shape[1]
    half = CONV_WIDTH // 2
    up = jnp.pad(u, ((0, 0), (half, half), (0, 0)))
    out = b
    for j in range(CONV_WIDTH):
        out = out + up[:, j:j + S] * w[j]
    return out


def encoder_layer(x, c, ada_w, ada_b, norm1_g, w_in, q_norm_g, kv_norm_g, w_uq, w_ukv,
                  p_a, p_b, w_out, norm2_g, w_up, conv_w, conv_b, w_down):
    B, S, D = x.shape
    mod = jnp.einsum('bd,de->be', jax.nn.silu(c), ada_w) + ada_b
    sh1, sc1, gt1, sh2, sc2, gt2 = jnp.split(mod, 6, axis=-1)

    h = rmsnorm(x, norm1_g) * (1 + sc1[:, None]) + sh1[:, None]
    z = jnp.einsum('bsd,de->bse', h, w_in)
    cuts = list(np.cumsum(IN_SPLITS)[:-1])
    c_q, c_kv, k_r, qkv_d, gate_a, gate_b = jnp.split(z, cuts, axis=-1)

    q = jnp.einsum('bsr,re->bse', rmsnorm(c_q, q_norm_g), w_uq).reshape(B, S, MLA_HEADS, QK_NOPE + QK_ROPE)
    q_nope, q_rope = q[..., :QK_NOPE], q[..., QK_NOPE:]
    kv = jnp.einsum('bsr,re->bse', rmsnorm(c_kv, kv_norm_g), w_ukv).reshape(B, S, MLA_HEADS, QK_NOPE + V_DIM)
    k_nope, v_a = kv[..., :QK_NOPE], kv[..., QK_NOPE:]
    cos, sin = rope_cos_sin(S)
    q_rope = apply_rope(q_rope, cos[:, None, :], sin[:, None, :])
    k_rope = apply_rope(k_r, cos, sin)
    o_a = mla_attention(q_nope, q_rope, k_nope, k_rope, v_a)

    qkv = qkv_d.reshape(B, S, 3, DIL_HEADS, DIL_HEAD_DIM)
    slopes = 2.0 ** (-8.0 * jnp.arange(1, DIL_HEADS + 1, dtype=jnp.float32) / DIL_HEADS)
    outs, lses = [], []
    for g, (window, dilation) in enumerate(DIL_CONFIGS):
        hs = slice(g * DIL_HEADS_PER_GROUP, (g + 1) * DIL_HEADS_PER_GROUP)
        o_g, lse_g = dilated_group_attention(qkv[:, :, 0, hs], qkv[:, :, 1, hs], qkv[:, :, 2, hs],
                                             slopes[hs], window, dilation)
        outs.append(o_g)
        lses.append(lse_g)
    wts = jax.nn.softmax(jnp.stack(lses, axis=0), axis=0)
    o_b = jnp.sum(wts[..., None].astype(x.dtype) * jnp.stack(outs, axis=0), axis=0)
    o_b = o_b.reshape(B, S, DIL_HEADS_PER_GROUP * DIL_HEAD_DIM)

    g_a = jax.nn.sigmoid(gate_a.astype(jnp.float32)).astype(x.dtype)
    g_b = jax.nn.sigmoid(gate_b.astype(jnp.float32)).astype(x.dtype)
    merged = g_a * jnp.einsum('bse,ed->bsd', o_a, p_a) + g_b * jnp.einsum('bse,ed->bsd', o_b, p_b)
    x = x + gt1[:, None] * jnp.einsum('bsd,de->bse', merged, w_out)

    h2 = rmsnorm(x, norm2_g) * (1 + sc2[:, None]) + sh2[:, None]
    up = jnp.einsum('bsd,df->bsf', h2, w_up)
    u, gv = jnp.split(up, 2, axis=-1)
    u = dwconv_centered(u, conv_w, conv_b)
    ff = jnp.einsum('bsf,fd->bsd', jax.nn.gelu(u) * gv, w_down)
    x = x + gt2[:, None] * ff
    return x


def setup_inputs(seed: int = 0) -> dict:
    key = jax.random.key(seed)
    ks = jax.random.split(key, 24)
    f32 = jnp.float32

    def nrm(k, shape, fan_in):
        return jax.random.normal(k, shape, f32) * (fan_in ** -0.5)

    def gain(k, shape):
        return 1.0 + 0.05 * jax.random.normal(k, shape, f32)

    L = DEPTH
    return {
        'x_prompt': jax.random.normal(ks[0], (BATCH, SEQ, D_MODEL), f32),
        'x_sample': jax.random.normal(ks[1], (DEC_BATCH, DEC_SEQ, D_MODEL), f32),
        'c_prompt': jax.random.normal(ks[2], (BATCH, D_MODEL), f32),
        'c_sample': jax.random.normal(ks[3], (DEC_BATCH, D_MODEL), f32),
        'ada_w': nrm(ks[4], (L, D_MODEL, 6 * D_MODEL), D_MODEL),
        'ada_b': 0.02 * jax.random.normal(ks[5], (L, 6 * D_MODEL), f32),
        'norm1_g': gain(ks[6], (L, D_MODEL)),
        'w_in': nrm(ks[7], (L, D_MODEL, IN_COLS), D_MODEL),
        'q_norm_g': gain(ks[8], (L, Q_LORA)),
        'kv_norm_g': gain(ks[9], (L, KV_LORA)),
        'w_uq': nrm(ks[10], (L, Q_LORA, MLA_HEADS * (QK_NOPE + QK_ROPE)), Q_LORA),
        'w_ukv': nrm(ks[11], (L, KV_LORA, MLA_HEADS * (QK_NOPE + V_DIM)), KV_LORA),
        'p_a': nrm(ks[12], (L, MLA_HEADS * V_DIM, D_MODEL), MLA_HEADS * V_DIM),
        'p_b': nrm(ks[13], (L, DIL_HEADS_PER_GROUP * DIL_HEAD_DIM, D_MODEL), DIL_HEADS_PER_GROUP * DIL_HEAD_DIM),
        'w_out': nrm(ks[14], (L, D_MODEL, D_MODEL), D_MODEL),
        'norm2_g': gain(ks[15], (L, D_MODEL)),
        'w_up': nrm(ks[16], (L, D_MODEL, 2 * D_FF), D_MODEL),
        'conv_w': nrm(ks[17], (L, CONV_WIDTH, D_FF), CONV_WIDTH),
        'conv_b': 0.02 * jax.random.normal(ks[18], (L, D_FF), f32),
        'w_down': nrm(ks[19], (L, D_FF, D_MODEL), D_FF),
        'normf_g': gain(ks[20], (D_MODEL,)),
    }


def reference(x_prompt, x_sample, c_prompt, c_sample, ada_w, ada_b, norm1_g, w_in, q_norm_g, kv_norm_g,
              w_uq, w_ukv, p_a, p_b, w_out, norm2_g, w_up, conv_w, conv_b, w_down, normf_g):
    def trunk(x, c):
        for l in range(DEPTH):
            x = encoder_layer(x, c, ada_w[l], ada_b[l], norm1_g[l], w_in[l], q_norm_g[l], kv_norm_g[l],
                              w_uq[l], w_ukv[l], p_a[l], p_b[l], w_out[l], norm2_g[l], w_up[l],
                              conv_w[l], conv_b[l], w_down[l])
        return rmsnorm(x, normf_g)

    y_prompt = trunk(x_prompt, c_prompt)
    y_sample = trunk(x_sample, c_sample)
    return (y_prompt, y_sample)
```

```python
import math
from contextlib import ExitStack
import numpy as np
import concourse.bass as bass
import concourse.mybir as mybir
from concourse.bass_utils import run_bass_kernel_spmd

F32 = mybir.dt.float32
BF16 = mybir.dt.bfloat16
AF = mybir.ActivationFunctionType
ALU = mybir.AluOpType
AX = mybir.AxisListType

D = 1024
KC = 8
DFF = 2816
FC = 22
EPS = 1e-6
NEG = -30000.0
DILS = (1, 4, 16)
MLA_SCALE = 96.0 ** -0.5
IN_COLS = 5024
GATE0 = 2976
DQ0 = 672


def I(name, *args, **kw):
    return lambda e: getattr(e, name)(*args, **kw)


def split(a, b, m):
    n = b - a
    if n <= 0:
        return []
    k = (n + m - 1) // m
    base, rem = divmod(n, k)
    out = []
    lo = a
    for i in range(k):
        sz = base + (1 if i < rem else 0)
        out.append((lo, lo + sz))
        lo += sz
    return out


class Res:
    __slots__ = ("name", "w", "r", "dsem", "dcnt")

    def __init__(self, name):
        self.name = name
        self.w = None
        self.r = []
        self.dsem = None
        self.dcnt = 0


class Op:
    __slots__ = ("eng", "fn", "waits", "signal", "val", "sem", "isdma")


class Sched:
    ATTR = {"pe": "tensor", "act": "scalar", "dve": "vector", "pool": "gpsimd", "sp": "sync"}

    def __init__(self, nc, stack):
        self.nc = nc
        self.stack = stack
        self.esem = {e: stack.enter_context(nc.semaphore("sem_" + e)) for e in self.ATTR}
        self.ecount = {e: 0 for e in self.ATTR}
        self.ops = {e: [] for e in self.ATTR}
        self.res_all = []
        self.nops = 0
        self.dpool = []
        self.dused = []
        self.nds = 0

    def res(self, name="r"):
        r = Res(name)
        self.res_all.append(r)
        return r

    def _record(self, op, reads, writes):
        deps = []
        for r in reads:
            if r.w is not None:
                deps.append((r.w, True))
        for r in writes:
            if r.w is not None:
                deps.append((r.w, False))
            for x in r.r:
                deps.append((x, False))
        for d, raw in deps:
            if d is op:
                continue
            if d.eng == op.eng and not d.isdma and not op.isdma:
                if op.eng == "pe":
                    continue
                if not raw:
                    continue
            if d not in op.waits:
                if not d.isdma:
                    d.signal = True
                op.waits.append(d)
        for r in reads:
            r.r.append(op)
        for r in writes:
            r.w = op
            r.r = []
        self.ops[op.eng].append(op)
        self.nops += 1

    def op(self, eng, fn, reads=(), writes=()):
        o = Op()
        o.eng = eng
        o.fn = fn
        o.waits = []
        o.signal = False
        o.val = None
        o.sem = None
        o.isdma = False
        self._record(o, reads, writes)
        return o

    def dma(self, eng, out, in_, semres, reads=(), writes=(), **kw):
        o = Op()
        o.eng = eng
        o.waits = []
        o.signal = True
        o.isdma = True
        if semres.dsem is None:
            if self.dpool:
                semres.dsem = self.dpool.pop()
            else:
                self.nds += 1
                semres.dsem = [self.stack.enter_context(self.nc.semaphore("dsem_%d" % self.nds)), 0]
            self.dused.append(semres)
        semres.dsem[1] += 16
        o.sem = semres.dsem[0]
        o.val = semres.dsem[1]
        o.fn = I("dma_start", out=out, in_=in_, **kw)
        self._record(o, reads, writes)
        return o

    def run_block(self):
        for e, lst in self.ops.items():
            for o in lst:
                if not o.isdma and o.signal:
                    self.ecount[e] += 1
                    o.val = self.ecount[e]
                    o.sem = self.esem[e]
        with self.nc.Block(no_gpsimd_drain=True) as blk:
            for e in self.ATTR:
                lst = self.ops[e]
                if not lst:
                    continue

                def body(eng, lst=lst):
                    waited = {}
                    finals = {}
                    for o in lst:
                        for d in o.waits:
                            k = id(d.sem)
                            if waited.get(k, -1) >= d.val:
                                continue
                            eng.wait_ge(d.sem, d.val)
                            waited[k] = d.val
                        ins = o.fn(eng)
                        if o.isdma:
                            ins.then_inc(o.sem, 16)
                            finals[id(o.sem)] = (o.sem, o.val)
                        elif o.signal:
                            ins.then_inc(o.sem, 1)
                    for sem, val in finals.values():
                        eng.wait_ge(sem, val)

                getattr(blk, self.ATTR[e])(body)
        self.ops = {e: [] for e in self.ATTR}
        for r in self.res_all:
            r.w = None
            r.r = []
        for r in self.dused:
            self.dpool.append(r.dsem)
            r.dsem = None
        self.dused = []


class T:
    CNT = [0]

    def __init__(self, S, stack, nc, name, shape, dtype, psum=False):
        T.CNT[0] += 1
        name = "%s_%d" % (name, T.CNT[0])
        if psum:
            self.t = stack.enter_context(nc.psum_tensor(name, shape, dtype))
        else:
            self.t = stack.enter_context(nc.sbuf_tensor(name, shape, dtype))
        self.r = S.res(name)

    def __getitem__(self, k):
        return self.t[k]


def make_units(cfg):
    SA, SB, QB, UQ = cfg["SA"], cfg["SB"], cfg["QB"], cfg["UQ"]
    units = []
    for seq, base, Sk, Qn, yb in ((0, 0, SA, SA, 0), (1, SA, SB, QB, SA)):
        for q0 in range(0, Qn, UQ):
            qa = max(0, q0 - 1)
            qb = min(Sk, q0 + UQ + 1)
            ra = max(0, qa - 1024)
            rb = min(Sk, qb + 1024)
            units.append(dict(seq=seq, base=base, Sk=Sk, q0=q0, q1=q0 + UQ, qa=qa, qb=qb, ra=ra, rb=rb,
                              first=(q0 == 0), ybase=yb + q0))
    return units


def build(cfg):
    SA, SB, QB, UQ = cfg["SA"], cfg["SB"], cfg["QB"], cfg["UQ"]
    NTOK = SA + SB
    NOUT = SA + QB
    SKMAX = max(SA, SB)
    NQM = UQ + 2
    NRM = UQ + 2 + 2048
    nc = bass.Bass("TRN2", target_bir_lowering=False)

    def din(name, shape, dt=F32):
        return nc.dram_tensor(name, list(shape), dt, kind="ExternalInput").ap()

    x = din("x", [NTOK, D])
    cT_d = din("cT", [2, 128, KC])
    ada_w = din("ada_w", [D, 6 * D])
    ada_b = din("ada_b", [6 * D])
    g1_d = din("norm1_g", [D])
    w_in = din("w_in", [D, IN_COLS])
    gq_d = din("gq", [128, 3])
    gkv_d = din("gkv", [128, 2])
    w_uq = din("w_uq", [384, 1536])
    w_ukv = din("w_ukv", [256, 2048])
    p_a = din("p_a", [1024, 1024])
    p_b = din("p_b", [256, 1024])
    w_out = din("w_out", [1024, 1024])
    g2_d = din("norm2_g", [D])
    w_up = din("w_up", [D, 2 * DFF])
    convw_d = din("convw", [2, 128, FC, 3])
    convb_d = din("convb", [128, FC])
    w_down = din("w_down", [DFF, D])
    gf_d = din("normf_g", [D])
    ropeA = din("ropeA", [32, 2, SA])
    ropeB = din("ropeB", [32, 2, SB])
    ident_d = din("ident", [128, 128])
    sel_d = din("sel", [128, 64])
    arel_d = din("arel", [128, 256])
    mband_d = din("mband", [128, 256])
    y = nc.dram_tensor("y", [NOUT, D], F32, kind="ExternalOutput").ap()
    x1s = nc.dram_tensor("x1s", [NQM, D], F32, kind="Internal").ap()
    wg_bf = nc.dram_tensor("wg_bf", [D, 2048], BF16, kind="Internal").ap()
    kscr = nc.dram_tensor("kscr", [16, 64, SKMAX], BF16, kind="Internal").ap()
    vscr = nc.dram_tensor("vscr", [16, 128, (SKMAX // 128) * 65], BF16, kind="Internal").ap()
    pa_bf = nc.dram_tensor("pa_bf", [1024, 1024], BF16, kind="Internal").ap()
    pb_bf = nc.dram_tensor("pb_bf", [256, 1024], BF16, kind="Internal").ap()
    wo_bf = nc.dram_tensor("wo_bf", [1024, 1024], BF16, kind="Internal").ap()
    wup_arr = nc.dram_tensor("wup_arr", [FC, 128, KC, 256], BF16, kind="Internal").ap()
    wdn_bf = nc.dram_tensor("wdn_bf", [DFF, D], BF16, kind="Internal").ap()
    ropes = (ropeA, ropeB)

    units = make_units(cfg)

    with ExitStack() as top:
        S = Sched(nc, top)

        def mk(stack, name, shape, dt, psum=False):
            return T(S, stack, nc, name, shape, dt, psum)

        ident = mk(top, "ident", [128, 128], BF16)
        sel = mk(top, "sel", [128, 64], F32)
        arel = mk(top, "arel", [128, 256], F32)
        mband = mk(top, "mband", [128, 256], F32)
        eps_t = mk(top, "eps_t", [128, 1], F32)
        ones_bf = mk(top, "ones_bf", [128, 128], BF16)
        ones_f = mk(top, "ones_f", [128, 128], F32)
        gf_bc = mk(top, "gf_bc", [128, D], F32)
        gq = mk(top, "gq", [128, 3], F32)
        gkv = mk(top, "gkv", [128, 2], F32)
        convw = mk(top, "convw", [128, FC, 3], F32)
        convb = mk(top, "convb", [128, FC], F32)
        modbc = mk(top, "modbc", [128, 6, D], F32)
        ckvnT = mk(top, "ckvnT", [128, 2, SKMAX], BF16)
        KT = mk(top, "KT", [96, SKMAX], BF16)

        S.dma("pool", ident[:], ident_d[:, :], ident.r, writes=[ident.r])
        S.dma("sp", sel[:], sel_d[:, :], sel.r, writes=[sel.r])
        S.dma("sp", arel[:], arel_d[:, :], arel.r, writes=[arel.r])
        S.dma("sp", mband[:], mband_d[:, :], mband.r, writes=[mband.r])
        S.dma("sp", gq[:], gq_d[:, :], gq.r, writes=[gq.r])
        S.dma("sp", gkv[:], gkv_d[:, :], gkv.r, writes=[gkv.r])
        S.dma("sp", convb[:], convb_d[:, :], convb.r, writes=[convb.r])
        S.dma("sp", gf_bc[:], gf_d.partition_broadcast(128), gf_bc.r, writes=[gf_bc.r])
        S.op("dve", I("memset", eps_t[:], EPS), writes=[eps_t.r])
        S.op("dve", I("memset", ones_bf[:], 1.0), writes=[ones_bf.r])
        S.op("dve", I("memset", ones_f[:], 1.0), writes=[ones_f.r])
        S.run_block()

        def emit_hT(st, u, tiles, Gi, SHi, dst_fn, dres, tag):
            ctxs = []
            for i, (lo, hi) in enumerate(tiles):
                nt = hi - lo
                xs = st["xs"][st["xi"] % len(st["xs"])]
                st["xi"] += 1
                S.dma("sp", xs[0:nt, :], x[u["base"] + lo:u["base"] + hi, :], xs.r, writes=[xs.r])
                ctxs.append(emit_normA(st, xs, nt))
                if i >= 1:
                    plo, phi = tiles[i - 1]
                    emit_normB(st, ctxs[i - 1], Gi, SHi, dst_fn(plo, phi), dres)
            if tiles:
                plo, phi = tiles[-1]
                emit_normB(st, ctxs[-1], Gi, SHi, dst_fn(plo, phi), dres)

        def emit_normA(st, xs, nt):
            k_ = st["pti"] % 2
            st["pti"] += 1
            hn, hb = st["hn"][k_], st["hb"][k_]
            ss, sd, rs = st["ss"][k_], st["sd"][k_], st["rs"][k_]
            sq = hn
            pt = st["pt"][k_ % len(st["pt"])]
            S.op("act", I("activation", out=sq[0:nt, :], in_=xs[0:nt, :], func=AF.Square),
                 reads=[xs.r], writes=[sq.r])
            S.op("dve", I("tensor_reduce", out=ss[0:nt, 0:1], in_=sq[0:nt, :], axis=AX.X, op=ALU.add),
                 reads=[sq.r], writes=[ss.r])
            S.op("act", I("activation", out=sd[0:nt, 0:1], in_=ss[0:nt, 0:1], func=AF.Ln,
                          bias=eps_t[0:nt, 0:1], scale=1.0 / D),
                 reads=[ss.r, eps_t.r], writes=[sd.r])
            S.op("act", I("activation", out=rs[0:nt, 0:1], in_=sd[0:nt, 0:1], func=AF.Exp, scale=-0.5), reads=[sd.r], writes=[rs.r])
            return (xs, nt, hn, hb, rs, pt)

        def emit_normB(st, ctx, Gi, SHi, dst, dres):
            xs, nt, hn, hb, rs, pt = ctx
            S.op("dve", I("scalar_tensor_tensor", out=hn[0:nt, :], in0=xs[0:nt, :], scalar=rs[0:nt, 0:1],
                          in1=modbc[0:nt, Gi, :], op0=ALU.mult, op1=ALU.mult),
                 reads=[xs.r, rs.r, modbc.r], writes=[hn.r])
            S.op("dve", I("tensor_tensor", out=hb[0:nt, :], in0=hn[0:nt, :], in1=modbc[0:nt, SHi, :],
                          op=ALU.add),
                 reads=[hn.r, modbc.r], writes=[hb.r])

            def tr(e):
                ins = None
                for c in range(KC):
                    ins = e.transpose(out=pt[:, c, 0:nt], in_=hb[0:nt, c * 128:(c + 1) * 128],
                                      identity=ident[0:nt, 0:nt])
                return ins

            S.op("pe", tr, reads=[hb.r, ident.r], writes=[pt.r])
            S.op("act", I("activation", out=dst, in_=pt[:, :, 0:nt], func=AF.Copy),
                 reads=[pt.r], writes=[dres])

        def emit_norm(st, xs, nt, Gi, SHi, dst, dres):
            emit_normB(st, emit_normA(st, xs, nt), Gi, SHi, dst, dres)

        def mm_acc(ps_ap, lhs_fn, rhs_fn, nk):
            pairs = [(lhs_fn(c), rhs_fn(c)) for c in range(nk)]

            def f(e):
                ins = None
                for c, (l_, r_) in enumerate(pairs):
                    ins = e.matmul(ps_ap, l_, r_, start=(c == 0), stop=(c == nk - 1))
                return ins
            return f

        def wload(wt, dram_ap, eng="pool"):
            S.dma(eng, wt, dram_ap.rearrange("(c p) n -> p c n", p=128), wt_res[0], writes=[wt_res[0]])

        wt_res = [None]

        cur_seq = -1
        for u in units:
            seq = u["seq"]
            Sk = u["Sk"]
            qa, qb, ra, rb, q0, q1 = u["qa"], u["qb"], u["ra"], u["rb"], u["q0"], u["q1"]
            nq = qb - qa
            nR = rb - ra
            rope = ropes[seq]

            if seq != cur_seq:
                cur_seq = seq
                with ExitStack() as ph:
                    cTt = mk(ph, "cTt", [128, KC], F32)
                    sg = mk(ph, "sg", [128, KC], F32)
                    sv = mk(ph, "sv", [128, KC], F32)
                    srep = mk(ph, "srep", [128, KC, 128], BF16)
                    aw = [mk(ph, "aw%d" % i, [128, KC, 512], BF16) for i in range(3)]
                    ab = [mk(ph, "ab%d" % i, [128, 512], F32) for i in range(2)]
                    gbc = [mk(ph, "gbc%d" % i, [128, D], F32) for i in range(2)]
                    pm = [mk(ph, "pm%d" % i, [128, 512], F32, psum=True) for i in range(2)]
                    S.dma("sp", cTt[:], cT_d[seq], cTt.r, writes=[cTt.r])
                    S.dma("sp", convw[:], convw_d[seq], convw.r, writes=[convw.r])
                    S.dma("sp", gbc[0][:], g1_d.partition_broadcast(128), gbc[0].r, writes=[gbc[0].r])
                    S.dma("sp", gbc[1][:], g2_d.partition_broadcast(128), gbc[1].r, writes=[gbc[1].r])
                    S.op("act", I("activation", out=sg[:], in_=cTt[:], func=AF.Sigmoid), reads=[cTt.r], writes=[sg.r])
                    S.op("dve", I("tensor_tensor", out=sv[:], in0=cTt[:], in1=sg[:], op=ALU.mult),
                         reads=[cTt.r, sg.r], writes=[sv.r])
                    for c in range(KC):
                        S.op("dve", I("tensor_scalar", out=srep[:, c, :], in0=ones_bf[:], scalar1=sv[:, c:c + 1],
                                                                   scalar2=None, op0=ALU.mult),
                             reads=[sv.r, ones_bf.r], writes=[srep.r])
                    for j in range(12):
                        a = aw[j % 3]
                        b = ab[j % 2]
                        p = pm[j % 2]
                        S.dma("pool", a[:], ada_w[:, j * 512:(j + 1) * 512].rearrange("(c p) n -> p c n", p=128), a.r, writes=[a.r])
                        S.dma("sp", b[:], ada_b[j * 512:(j + 1) * 512].partition_broadcast(128), b.r, writes=[b.r])
                        S.op("pe", mm_acc(p[:, :], lambda c: srep[:, c, :], lambda c, a=a: a[:, c, :], KC),
                             reads=[srep.r, a.r], writes=[p.r])
                        S.op("dve", I("tensor_tensor", out=modbc[:, j // 2, (j % 2) * 512:(j % 2) * 512 + 512], in0=p[:, :], in1=b[:, :], op=ALU.add),
                            reads=[p.r, b.r], writes=[modbc.r])
                    for gi, mi in ((0, 1), (1, 4)):
                        S.op("dve", I("scalar_tensor_tensor", out=modbc[:, mi, :], in0=modbc[:, mi, :], scalar=1.0, in1=gbc[gi][:, :],
                            op0=ALU.add, op1=ALU.mult), reads=[modbc.r, gbc[gi].r], writes=[modbc.r])
                    S.run_block()

            with ExitStack() as ust:
                p14 = ust.enter_context(ExitStack())
                cqnT = mk(p14, "cqnT", [128, 3, NQM], BF16)
                o_bT = mk(p14, "o_bT", [128, 2, NQM], BF16)

                with ExitStack() as p12:
                    hT_u = mk(p12, "hT_u", [128, KC, nR], BF16)
                    with ExitStack() as ph:
                        st = dict(xi=0, pti=0)
                        st["xs"] = [mk(ph, "xs%d" % i, [128, D], F32) for i in range(6 if (nR <= 2049 and not (u["first"] and rb < Sk)) else 5)]
                        st["hn"] = [mk(ph, "hn%d" % i, [128, D], F32) for i in range(2)]
                        st["hb"] = [mk(ph, "hb%d" % i, [128, D], BF16) for i in range(2)]
                        st["ss"] = [mk(ph, "ss%d" % i, [128, 1], F32) for i in range(2)]
                        st["sd"] = [mk(ph, "sd%d" % i, [128, 1], F32) for i in range(2)]
                        st["rs"] = [mk(ph, "rs%d" % i, [128, 1], F32) for i in range(2)]
                        st["pt"] = [mk(ph, "pt%d" % i, [128, KC, 128], BF16, psum=True) for i in range(2)]
                        psA = [mk(ph, "psA%d" % i, [128, 512], F32, psum=True) for i in range(3)]
                        psS = mk(ph, "psS", [128, 512], F32, psum=True)
                        psK = mk(ph, "psK", [128, 512], F32, psum=True)
                        psR = mk(ph, "psR", [128, 512], F32, psum=True)
                        wlat = mk(ph, "wlat", [128, KC, 704], BF16)
                        htmp = [mk(ph, "htmp%d" % i, [128, KC, 512], BF16) for i in range(2 if (u["first"] and rb < Sk) else 0)]
                        sqf = [mk(ph, "sqf%d" % i, [128, 512], F32) for i in range(3)]
                        sdt = sqf[2]
                        rsb = mk(ph, "rsb", [128, 512], F32)
                        csk = [mk(ph, "csk%d" % i, [32, 2, 512], F32) for i in range(1)]
                        t1 = rsb
                        t2 = sdt
                        hres = {}

                        win3 = w_in.rearrange("(c p) n -> p c n", p=128)
                        S.dma("pool", wlat[:, :, 0:672], win3[:, :, 0:672], wlat.r, writes=[wlat.r])
                        S.dma("pool", wlat[:, :, 672:688], win3[:, :, 656:672], wlat.r, writes=[wlat.r])
                        S.dma("pool", wlat[:, :, 688:704], win3[:, :, 640:656], wlat.r, writes=[wlat.r])
                        S.op("dve", I("tensor_scalar", out=wlat[:, :, 672:688], in0=wlat[:, :, 672:688], scalar1=-1.0,
                                                              scalar2=None, op0=ALU.mult), reads=[wlat.r], writes=[wlat.r])

                        segs = []
                        if u["first"]:
                            cuts = sorted(set([0, ra, qa, qb, rb, Sk]))
                        else:
                            cuts = sorted(set([ra, qa, qb, rb]))
                        for a_, b_ in zip(cuts[:-1], cuts[1:]):
                            segs.append((a_, b_))
                        def latents(blo, bhi, N, hr, hblk, inQ):
                            def rhsf(c):
                                return hblk[:, c, :]

                            def stat_norm(nch, col0, gt, outT, ocol, ores):
                                for m in range(nch):
                                    S.op("pe", mm_acc(psA[m][:, 0:N], lambda c, m=m: wlat[:, c, col0 + 128 * m:col0 + 128 * m + 128],
                                                      rhsf, KC), reads=[hr, wlat.r], writes=[psA[m].r])
                                    S.op("act", I("activation", out=sqf[m][:, 0:N], in_=psA[m][:, 0:N], func=AF.Square),
                                         reads=[psA[m].r], writes=[sqf[m].r])
                                S.op("pe", mm_acc(psS[:, 0:N], lambda c: ones_f[:, :], lambda c: sqf[c][:, 0:N], nch),
                                     reads=[ones_f.r] + [sqf[m].r for m in range(nch)], writes=[psS.r])
                                S.op("act", I("activation", out=sdt[:, 0:N], in_=psS[:, 0:N], func=AF.Ln,
                                              bias=eps_t[:, 0:1], scale=1.0 / (128 * nch)),
                                     reads=[psS.r, eps_t.r], writes=[sdt.r])
                                S.op("act", I("activation", out=rsb[:, 0:N], in_=sdt[:, 0:N], func=AF.Exp, scale=-0.5),
                                     reads=[sdt.r], writes=[rsb.r])
                                for m in range(nch):
                                    S.op("dve", I("scalar_tensor_tensor", out=outT[:, m, ocol:ocol + N], in0=psA[m][:, 0:N], scalar=gt[:, m:m + 1],
                                                  in1=rsb[:, 0:N], op0=ALU.mult, op1=ALU.mult),
                                         reads=[psA[m].r, gt.r, rsb.r], writes=[ores])

                            if u["first"]:
                                stat_norm(2, 384, gkv, ckvnT, blo, S.res("ckv"))
                                ck = csk[0]
                                S.dma("pool", ck[:, :, 0:N], rope[:, :, blo:bhi], ck.r, writes=[ck.r])
                                S.op("pe", mm_acc(psK[0:32, 0:N], lambda c: wlat[:, c, 640:672], rhsf, KC),
                                     reads=[hr, wlat.r], writes=[psK.r])
                                S.op("pe", mm_acc(psR[0:32, 0:N], lambda c: wlat[:, c, 672:704], rhsf, KC),
                                     reads=[hr, wlat.r], writes=[psR.r])
                                S.op("dve", I("tensor_tensor", out=t1[0:32, 0:N], in0=psR[0:32, 0:N], in1=ck[:, 1, 0:N], op=ALU.mult),
                                     reads=[psR.r, ck.r], writes=[t1.r])
                                S.op("dve", I("tensor_tensor", out=t2[0:32, 0:N], in0=psK[0:32, 0:N], in1=ck[:, 0, 0:N], op=ALU.mult),
                                     reads=[psK.r, ck.r], writes=[t2.r])
                                S.op("dve", I("tensor_tensor", out=KT[64:96, blo:bhi], in0=t1[0:32, 0:N], in1=t2[0:32, 0:N], op=ALU.add),
                                     reads=[t1.r, t2.r], writes=[S.res("ktr")])
                            if inQ:
                                stat_norm(3, 0, gq, cqnT, blo - qa, S.res("cq"))

                        bi = 0
                        deferred = []
                        for (sa_, sb_) in segs:
                            inR = (sa_ >= ra and sb_ <= rb)
                            inQ = (sa_ >= qa and sb_ <= qb)
                            for (blo, bhi) in split(sa_, sb_, 512):
                                N = bhi - blo
                                if inR:
                                    hr = S.res("hres")
                                    dstf = lambda lo, hi: hT_u[:, :, lo - ra:hi - ra]
                                    hblk = hT_u[:, :, blo - ra:bhi - ra]
                                else:
                                    hb_ = htmp[bi % len(htmp)]
                                    hr = hb_.r
                                    dstf = lambda lo, hi, hb_=hb_, blo=blo: hb_[:, :, lo - blo:hi - blo]
                                    hblk = hb_[:, :, 0:N]
                                bi += 1
                                emit_hT(st, u, split(blo, bhi, 128), 1, 0, dstf, hr, "p1")
                                if u["first"] or inQ:
                                    deferred.append((blo, bhi, N, hr, hblk, inQ))
                                if len(deferred) > 1:
                                    latents(*deferred.pop(0))
                        while deferred:
                            latents(*deferred.pop(0))
                        S.run_block()

                    with ExitStack() as ph:
                        wd = [mk(ph, "wd%d" % i, [128, KC, 384], BF16) for i in range(2)]
                        qTs_ = [mk(ph, "qTd%d" % i, [128, NQM], BF16) for i in range(2)]
                        kTs_ = [mk(ph, "kTd%d" % i, [128, nR], BF16) for i in range(2)]
                        NTV = 32
                        Vds_ = [mk(ph, "Vd%d" % i, [128, NTV, 2, 65], BF16) for i in range(2)]
                        Uacc = mk(ph, "Uacc", [65, 2, NQM], F32)
                        biasT = [mk(ph, "biasT%d" % i, [128, 256], F32) for i in range(2)]
                        sbt = [mk(ph, "sbt%d" % i, [128, 256], F32) for i in range(3)]
                        Pt = [mk(ph, "Ptd%d" % i, [128, 256], BF16) for i in range(3)]
                        pj = [mk(ph, "pj%d" % i, [128, 512], F32, psum=True) for i in range(2)]
                        pS = [mk(ph, "pS%d" % i, [128, 512], F32, psum=True) for i in range(3)]
                        pO = [mk(ph, "pO%d" % i, [128, 512], F32, psum=True) for i in range(2)]
                        pL = pj[0]
                        vT = mk(ph, "vTd", [128, nR], BF16)
                        cnt_t = 0
                        win3 = w_in.rearrange("(c p) n -> p c n", p=128)
                        for Vd in Vds_:
                            S.op("dve", I("memset", Vd[:], 1.0), writes=[Vd.r])
                        cnt_pj = 0
                        cnt_s = 0
                        cnt_o = 0
                        cnt_b = 0
                        cnt_w = 0

                        def pair_task(pp, g):
                            nonlocal cnt_pj, cnt_s, cnt_o, cnt_b, cnt_w, cnt_t
                            if True:
                                d = DILS[g]
                                hd0 = 4 * g + 2 * pp
                                w = wd[cnt_w % 2]
                                qT, kT, Vd = qTs_[cnt_w % 2], kTs_[cnt_w % 2], Vds_[cnt_w % 2]
                                cnt_w += 1
                                for k3 in range(3):
                                    c0 = DQ0 + 768 * k3 + 64 * hd0
                                    S.dma("pool", w[:, :, 128 * k3:128 * k3 + 128], win3[:, :, c0:c0 + 128], w.r, writes=[w.r])
                                for (blo, bhi) in split(qa, qb, 512):
                                    N = bhi - blo
                                    p = pj[cnt_pj % 2]
                                    cnt_pj += 1
                                    S.op("pe", mm_acc(p[:, 0:N], lambda c, w=w: w[:, c, 0:128],
                                                      lambda c, blo=blo, bhi=bhi: hT_u[:, c, blo - ra:bhi - ra], KC),
                                         reads=[w.r], writes=[p.r])
                                    S.op("act", I("activation", out=qT[:, blo - qa:bhi - qa], in_=p[:, 0:N], func=AF.Copy, scale=0.125),
                                        reads=[p.r], writes=[qT.r])
                                for (blo, bhi) in split(ra, rb, 512):
                                    N = bhi - blo
                                    p = pj[cnt_pj % 2]
                                    cnt_pj += 1
                                    S.op("pe", mm_acc(p[:, 0:N], lambda c, w=w: w[:, c, 128:256],
                                                      lambda c, blo=blo, bhi=bhi: hT_u[:, c, blo - ra:bhi - ra], KC),
                                         reads=[w.r], writes=[p.r])
                                    S.op("dve", I("tensor_copy", out=kT[:, blo - ra:bhi - ra], in_=p[:, 0:N]), reads=[p.r], writes=[kT.r])
                                tiles = []
                                for r in range(d):
                                    L = (Sk - r + d - 1) // d
                                    nql = max(0, -((r - qa) // d))
                                    nqh = min(L, -((r - qb) // d))
                                    if nqh <= nql:
                                        continue
                                    klo = max(0, nql - 64)
                                    khi = min(L, nqh + 64)
                                    for (ks, ke) in [(s_, min(s_ + 128, khi)) for s_ in range(klo, khi, 128)]:
                                        ql = max(ks - 64, nql)
                                        qh = min(ke + 64, nqh)
                                        if qh <= ql:
                                            continue
                                        tiles.append((r, ks, ke - ks, ql, qh - ql))
                                assert len(tiles) <= NTV, len(tiles)
                                for (blo, bhi) in split(ra, rb, 512):
                                    N = bhi - blo
                                    p = pj[cnt_pj % 2]
                                    cnt_pj += 1
                                    S.op("pe", mm_acc(p[:, 0:N], lambda c, w=w: w[:, c, 256:384],
                                                      lambda c, blo=blo, bhi=bhi: hT_u[:, c, blo - ra:bhi - ra], KC),
                                         reads=[w.r], writes=[p.r])
                                    S.op("act", I("activation", out=vT[:, blo - ra:bhi - ra], in_=p[:, 0:N], func=AF.Copy), reads=[p.r], writes=[vT.r])
                                for ti, (r, ks, cnt, ql, nqq) in enumerate(tiles):
                                    p = pj[cnt_pj % 2]
                                    cnt_pj += 1
                                    pv_ = p[0:cnt, 0:64].bitcast(BF16)
                                    tk0 = r + d * ks - ra
                                    S.op("pe", I("transpose", out=pv_, in_=vT[:, tk0:tk0 + d * (cnt - 1) + 1:d], identity=ident[:, :]),
                                         reads=[vT.r, ident.r], writes=[p.r])
                                    S.op("dve", I("tensor_copy", out=Vd[0:cnt, ti, :, 0:64], in_=pv_.rearrange("p (a b) -> p a b", a=2)),
                                         reads=[p.r], writes=[Vd.r])
                                yield
                                for e_ in range(2):
                                    hd = hd0 + e_
                                    slope = 2.0 ** (-8.0 * (hd + 1) / 12.0)
                                    bt = biasT[cnt_b % 2]
                                    cnt_b += 1
                                    S.op("dve", I("scalar_tensor_tensor", out=bt[:, :], in0=arel[:, :], scalar=-slope * d, in1=mband[:, :], op0=ALU.mult, op1=ALU.add),
                                        reads=[arel.r, mband.r], writes=[bt.r])
                                    pb = 64 * e_
                                    pendd = []

                                    def st2(args):
                                        (po, P_, ti, cnt, nqq, qsl) = args
                                        S.op("pe", I("matmul", po[0:65, 0:nqq], Vd[0:cnt, ti, e_, 0:65], P_[0:cnt, 0:nqq], start=True, stop=True),
                                             reads=[Vd.r, P_.r], writes=[po.r])
                                        S.op("dve", I("tensor_tensor", out=Uacc[0:65, e_, qsl], in0=Uacc[0:65, e_, qsl], in1=po[0:65, 0:nqq], op=ALU.add),
                                             reads=[po.r, Uacc.r], writes=[Uacc.r])

                                    for ti, (r, ks, cnt, ql, nqq) in enumerate(tiles):
                                        ps_ = pS[cnt_s % 3]
                                        sb_ = sbt[cnt_s % 3]
                                        P_ = Pt[cnt_s % 3]
                                        cnt_s += 1
                                        po = pO[cnt_o % 2]
                                        cnt_o += 1
                                        tk0 = r + d * ks - ra
                                        tq0 = r + d * ql - qa
                                        c0 = 64 - (ks - ql)
                                        ksl = slice(tk0, tk0 + d * (cnt - 1) + 1, d)
                                        qsl = slice(tq0, tq0 + d * (nqq - 1) + 1, d)
                                        S.op("pe", I("matmul", ps_[0:cnt, 0:nqq], kT[pb:pb + 64, ksl], qT[pb:pb + 64, qsl], start=True, stop=True),
                                            reads=[kT.r, qT.r], writes=[ps_.r])
                                        S.op("dve", I("tensor_tensor", out=sb_[0:cnt, 0:nqq], in0=ps_[0:cnt, 0:nqq], in1=bt[0:cnt, c0:c0 + nqq], op=ALU.add),
                                            reads=[ps_.r, bt.r], writes=[sb_.r])
                                        S.op("act", I("activation", out=P_[0:cnt, 0:nqq], in_=sb_[0:cnt, 0:nqq], func=AF.Exp),
                                            reads=[sb_.r], writes=[P_.r])
                                        pendd.append((po, P_, ti, cnt, nqq, qsl))
                                        if len(pendd) > 2:
                                            st2(pendd.pop(0))
                                    while pendd:
                                        st2(pendd.pop(0))
                                    if e_ == 0:
                                        yield
                        def normalize_pp(pp):
                            for e_ in range(2):
                                S.op("dve", I("reciprocal", out=Uacc[64:65, e_, 0:nq], in_=Uacc[64:65, e_, 0:nq]),
                                     reads=[Uacc.r], writes=[Uacc.r])
                                for (blo, bhi) in split(0, nq, 512):
                                    N = bhi - blo
                                    S.op("pe", I("matmul", pL[0:64, 0:N], sel[0:65, 0:64], Uacc[0:65, e_, blo:bhi], start=True, stop=True),
                                        reads=[sel.r, Uacc.r], writes=[pL.r])
                                    S.op("dve", I("tensor_tensor", out=o_bT[64 * e_:64 * e_ + 64, pp, blo:bhi], in0=Uacc[0:64, e_, blo:bhi],
                                        in1=pL[0:64, 0:N], op=ALU.mult), reads=[Uacc.r, pL.r], writes=[o_bT.r])

                        tasks = [(pp_, g_) for pp_ in range(2) for g_ in range(3)]
                        gens = [pair_task(pp_, g_) for (pp_, g_) in tasks]
                        S.op("dve", I("memset", Uacc[:], 0.0), writes=[Uacc.r])
                        next(gens[0])
                        for ti_ in range(len(tasks)):
                            next(gens[ti_])
                            if ti_ + 1 < len(tasks):
                                next(gens[ti_ + 1])
                            for _ in gens[ti_]:
                                pass
                            if tasks[ti_][1] == 2:
                                normalize_pp(tasks[ti_][0])
                                if ti_ + 1 < len(tasks):
                                    S.op("dve", I("memset", Uacc[:], 0.0), writes=[Uacc.r])
                        S.run_block()

                h2T = mk(ust, "h2T", [128, KC, NQM], BF16)
                with ExitStack() as p34:
                    o_aT = mk(p34, "o_aT", [128, 8, NQM], BF16)
                    with ExitStack() as ph:
                        wukv = mk(ph, "wukv", [128, 2, 2048], BF16)
                        wuq = mk(ph, "wuq", [128, 3, 1536], BF16)
                        wuqr = mk(ph, "wuqr", [128, 3, 16, 96], BF16)
                        csq = mk(ph, "csq", [96, 2, NQM], F32)
                        NKT = Sk // 128
                        KT2 = mk(ph, "KT2", [96, SKMAX], BF16)
                        KTs = [KT, KT2]
                        ktn = [S.res("ktn0"), S.res("ktn1")]
                        Vhs = [mk(ph, "Vh%d" % i, [128, SKMAX // 128, 65], BF16) for i in range(2)]
                        QTs = [mk(ph, "QT%d" % i, [96, NQM], BF16) for i in range(2)]
                        t1 = mk(ph, "t1q", [96, 512], F32)
                        t2 = mk(ph, "t2q", [96, 512], F32)
                        NP = 4
                        Pm = [mk(ph, "Pm%d" % i, [128, 2, 512], BF16) for i in range(NP)]
                        Osb = mk(ph, "Osb", [65, 512], F32)
                        pS = [mk(ph, "mS%d" % i, [128, 2, 512], F32, psum=True) for i in range(3)]
                        pO = [mk(ph, "mO%d" % i, [128, 512], F32, psum=True) for i in range(1)]
                        pB = [mk(ph, "mB%d" % i, [128, 512], F32, psum=True) for i in range(1)]
                        pL = pB[0]
                        S.dma("pool", wukv[:], w_ukv.rearrange("(c p) n -> p c n", p=128), wukv.r, writes=[wukv.r])
                        S.dma("pool", wuq[:], w_uq.rearrange("(c p) n -> p c n", p=128), wuq.r, writes=[wuq.r])
                        S.op("dve", I("memset", wuqr[:], 0.0), writes=[wuqr.r])
                        wq4 = w_uq.rearrange("(c p) (h e) -> p c h e", p=128, e=96)
                        for c in range(3):
                            S.dma("pool", wuqr[:, c, :, 64:80], wq4[:, c, :, 80:96], wuqr.r, writes=[wuqr.r])
                            S.dma("pool", wuqr[:, c, :, 80:96], wq4[:, c, :, 64:80], wuqr.r, writes=[wuqr.r])
                        S.op("dve", I("tensor_scalar", out=wuqr[:, :, :, 64:80], in0=wuqr[:, :, :, 64:80], scalar1=-1.0,
                                      scalar2=None, op0=ALU.mult), reads=[wuqr.r], writes=[wuqr.r])
                        S.dma("sp", csq[64:96, :, 0:nq], rope[:, :, qa:qb], csq.r, writes=[csq.r])
                        for b_ in range(2):
                            S.op("dve", I("memset", Vhs[b_][:], 1.0), writes=[Vhs[b_].r])
                        if u is units[0]:
                            for i_ in range(FC):
                                for hf_ in range(2):
                                    S.dma("pool", wup_arr[i_, :, :, 128 * hf_:128 * hf_ + 128],
                                          w_up[:, hf_ * DFF + 128 * i_:hf_ * DFF + 128 * i_ + 128].rearrange("(c p) n -> p c n", p=128),
                                          S.res("cv"))
                            for dst_, src_ in ((wg_bf, w_in[:, GATE0:GATE0 + 2048]), (pa_bf, p_a), (pb_bf, p_b), (wo_bf, w_out),
                                               (wdn_bf, w_down)):
                                rows = dst_.shape[0]
                                for (r0, r1) in split(0, rows, 512):
                                    S.dma("pool", dst_[r0:r1, :], src_[r0:r1, :], S.res("cv"))
                        kt2rope = S.res("kt2rope")
                        S.op("dve", I("tensor_copy", out=KT2[64:96, 0:Sk], in_=KT[64:96, 0:Sk]), writes=[kt2rope])
                        cnt3 = dict(cb=0, cs=0, co=0)
                        qblks = split(0, nq, 512)

                        def build_chunks(h, buf):
                            KTb, Vh, QT = KTs[buf], Vhs[buf], QTs[buf]
                            chunks = []
                            if not u["first"]:
                                def cl_():
                                    S.dma("sp", KTb[0:64, 0:Sk], kscr[h, :, 0:Sk], S.res("kld"), writes=[ktn[buf]])
                                chunks.append(cl_)
                            for kb in range(Sk // 512 if u["first"] else 0):
                                def ck_(kb=kb):
                                    p = pB[0]
                                    cnt3["cb"] += 1
                                    S.op("pe", mm_acc(p[0:64, 0:512], lambda c: wukv[:, c, 128 * h:128 * h + 64],
                                                      lambda c: ckvnT[:, c, kb * 512:kb * 512 + 512], 2),
                                         reads=[wukv.r], writes=[p.r])
                                    S.op("dve", I("tensor_copy", out=KTb[0:64, kb * 512:kb * 512 + 512], in_=p[0:64, 0:512]),
                                         reads=[p.r], writes=[ktn[buf]])
                                chunks.append(ck_)
                            for j0 in range(0, NKT if u["first"] else 0, 8):
                                def cv_(j0=j0):
                                    p = pB[0]
                                    cnt3["cb"] += 1
                                    gsz = min(8, NKT - j0)

                                    def vb(e):
                                        ins = None
                                        for jj in range(gsz):
                                            for c in range(2):
                                                ins = e.matmul(p[:, jj * 64:jj * 64 + 64],
                                                               ckvnT[:, c, (j0 + jj) * 128:(j0 + jj) * 128 + 128],
                                                               wukv[:, c, 128 * h + 64:128 * h + 128], start=(c == 0), stop=(c == 1))
                                        return ins
                                    S.op("pe", vb, reads=[wukv.r], writes=[p.r])
                                    S.op("dve", I("tensor_copy", out=Vh[:, j0:j0 + gsz, 0:64],
                                                  in_=p[:, 0:64 * gsz].rearrange("p (a b) -> p a b", a=gsz)),
                                         reads=[p.r], writes=[Vh.r])
                                chunks.append(cv_)
                            for (blo, bhi) in qblks:
                                def cq_(blo=blo, bhi=bhi):
                                    N = bhi - blo
                                    pq = pB[0]
                                    cnt3["cb"] += 1
                                    pr = pB[0]
                                    S.op("pe", mm_acc(pq[0:96, 0:N], lambda c: wuq[:, c, 96 * h:96 * h + 96],
                                                      lambda c: cqnT[:, c, blo:bhi], 3), reads=[wuq.r], writes=[pq.r])
                                    S.op("dve", I("tensor_copy", out=QT[0:64, blo:bhi], in_=pq[0:64, 0:N]), reads=[pq.r], writes=[QT.r])
                                    S.op("dve", I("tensor_tensor", out=t2[64:96, 0:N], in0=pq[64:96, 0:N], in1=csq[64:96, 0, blo:bhi], op=ALU.mult),
                                         reads=[pq.r, csq.r], writes=[t2.r])
                                    S.op("pe", mm_acc(pr[0:96, 0:N], lambda c: wuqr[:, c, h, :],
                                                      lambda c: cqnT[:, c, blo:bhi], 3), reads=[wuqr.r], writes=[pr.r])
                                    S.op("dve", I("tensor_tensor", out=t1[64:96, 0:N], in0=pr[64:96, 0:N], in1=csq[64:96, 1, blo:bhi], op=ALU.mult),
                                         reads=[pr.r, csq.r], writes=[t1.r])
                                    S.op("dve", I("tensor_tensor", out=QT[64:96, blo:bhi], in0=t1[64:96, 0:N], in1=t2[64:96, 0:N], op=ALU.add),
                                         reads=[t1.r, t2.r], writes=[QT.r])
                                chunks.append(cq_)
                            if not u["first"]:
                                def clv_():
                                    S.dma("sp", Vh[:, 0:NKT, :], vscr[h, :, 0:NKT * 65].rearrange("p (a b) -> p a b", b=65),
                                          S.res("vld"), writes=[Vh.r])
                                chunks.insert(2, clv_)
                            if u["first"] and len(units) > 1:
                                def cs_():
                                    S.dma("pool", kscr[h, :, 0:Sk], KTb[0:64, 0:Sk], S.res("kst"), reads=[ktn[buf]])
                                    S.dma("pool", vscr[h, :, 0:NKT * 65].rearrange("p (a b) -> p a b", b=65), Vh[:, 0:NKT, :],
                                          S.res("vst"), reads=[Vh.r])
                                chunks.append(cs_)
                            return chunks

                        for ch in build_chunks(0, 0):
                            ch()
                        m0 = q0 - qa
                        mblks = split(m0, m0 + (q1 - q0), 512)
                        hcols = ([0] if qa < q0 else []) + ([nq - 1] if qb > q1 else [])
                        nh = len(hcols)
                        hsl = slice(hcols[0], hcols[-1] + 1, (hcols[-1] - hcols[0]) if nh == 2 else 1) if nh else None
                        NG = NKT // 2
                        nsteps = len(mblks) * NG

                        po = pO[0]

                        OsbH = mk(ph, "OsbH", [65, 8], F32)

                        def normA(N, Osb=Osb):
                            S.op("dve", I("tensor_copy", out=Osb[0:65, 0:N], in_=po[0:65, 0:N]), reads=[po.r], writes=[Osb.r])
                            S.op("act", I("activation", out=Osb[64:65, 0:N], in_=Osb[64:65, 0:N], func=AF.Ln), reads=[Osb.r], writes=[Osb.r])
                            S.op("act", I("activation", out=Osb[64:65, 0:N], in_=Osb[64:65, 0:N], func=AF.Exp, scale=-1.0), reads=[Osb.r], writes=[Osb.r])

                        def normB(N, h, osl, Osb=Osb):
                            e_ = h % 2
                            S.op("pe", I("matmul", pL[0:64, 0:N], sel[0:65, 0:64], Osb[0:65, 0:N], start=True, stop=True),
                                 reads=[sel.r, Osb.r], writes=[pL.r])
                            S.op("dve", I("tensor_tensor", out=o_aT[64 * e_:64 * e_ + 64, h // 2, osl], in0=Osb[0:64, 0:N], in1=pL[0:64, 0:N], op=ALU.mult),
                                 reads=[Osb.r, pL.r], writes=[o_aT.r])

                        items = []
                        for h in range(16):
                            for (blo, bhi) in mblks:
                                for g in range(NG):
                                    items.append(("m", h, blo, bhi, g))
                            if nh:
                                items.append(("h", h, 0, 0, 0))
                        per_head = len(mblks) * NG + (1 if nh else 0)
                        state = {}

                        def s1(i):
                            kind, h, blo, bhi, g = items[i]
                            buf = h % 2
                            KTb, QT = KTs[buf], QTs[buf]
                            krd = [ktn[buf], QT.r] + ([kt2rope] if buf == 1 else [])
                            ps_ = pS[cnt3["cs"] % 3]
                            P_ = Pm[cnt3["cs"] % NP]
                            cnt3["cs"] += 1
                            state[i] = P_
                            if kind == "m":
                                N = bhi - blo

                                def f1(e):
                                    ins = None
                                    for j in range(2):
                                        kt = 2 * g + j
                                        ins = e.matmul(ps_[:, j, 0:N], KTb[0:96, kt * 128:kt * 128 + 128], QT[0:96, blo:bhi], start=True, stop=True)
                                    return ins
                                S.op("pe", f1, reads=krd, writes=[ps_.r])
                                S.op("act", I("activation", out=P_[:, :, 0:N], in_=ps_[:, :, 0:N], func=AF.Exp, scale=MLA_SCALE),
                                     reads=[ps_.r], writes=[P_.r])
                            else:
                                def fh(e):
                                    ins = None
                                    for kt in range(NKT):
                                        ins = e.matmul(ps_[:, 0, kt * nh:kt * nh + nh], KTb[0:96, kt * 128:kt * 128 + 128], QT[0:96, hsl], start=True, stop=True)
                                    return ins
                                S.op("pe", fh, reads=krd, writes=[ps_.r])
                                S.op("act", I("activation", out=P_[:, 0, 0:NKT * nh], in_=ps_[:, 0, 0:NKT * nh], func=AF.Exp, scale=MLA_SCALE),
                                     reads=[ps_.r], writes=[P_.r])

                        def s2(i):
                            kind, h, blo, bhi, g = items[i]
                            Vh = Vhs[h % 2]
                            P_ = state.pop(i)
                            if kind == "m":
                                N = bhi - blo

                                def f(e):
                                    ins = None
                                    for j in range(2):
                                        kt = 2 * g + j
                                        ins = e.matmul(po[0:65, 0:N], Vh[:, kt, 0:65], P_[:, j, 0:N], start=(kt == 0), stop=(kt == NKT - 1))
                                    return ins
                                S.op("pe", f, reads=[Vh.r, P_.r], writes=[po.r])
                                if g == NG - 1:
                                    normA(N)
                            else:
                                def fh2(e):
                                    ins = None
                                    for kt in range(NKT):
                                        ins = e.matmul(po[0:65, 0:nh], Vh[:, kt, 0:65], P_[:, 0, kt * nh:kt * nh + nh], start=(kt == 0), stop=(kt == NKT - 1))
                                    return ins
                                S.op("pe", fh2, reads=[Vh.r, P_.r], writes=[po.r])
                                normA(nh, OsbH)

                        def s3(i):
                            kind, h, blo, bhi, g = items[i]
                            if kind == "m":
                                if g == NG - 1:
                                    normB(bhi - blo, h, slice(blo, bhi))
                            else:
                                normB(nh, h, hsl, OsbH)

                        nxt = []
                        LA1, LA2 = 2, 3
                        n_it = len(items)
                        for i in range(n_it + LA1 + LA2):
                            if i < n_it:
                                h = items[i][1]
                                if i % per_head == 0 and h < 15:
                                    while nxt:
                                        nxt.pop(0)()
                                    nxt = build_chunks(h + 1, 1 - (h % 2))
                                    every = max(1, (per_head - 6) // max(1, len(nxt)))
                                s1(i)
                                if nxt and (i % per_head) % every == 0:
                                    nxt.pop(0)()
                            if 0 <= i - LA1 < n_it:
                                s2(i - LA1)
                            if 0 <= i - LA1 - LA2 < n_it:
                                s3(i - LA1 - LA2)
                        while nxt:
                            nxt.pop(0)()
                        S.run_block()

                    with ExitStack() as ph:
                        st = dict(xi=0, pti=0)
                        st["xs"] = [mk(ph, "xs%d" % i, [128, D], F32) for i in range(3)]
                        st["hn"] = [mk(ph, "hn%d" % i, [128, D], F32) for i in range(2)]
                        st["hb"] = [mk(ph, "hb%d" % i, [128, D], BF16) for i in range(2)]
                        st["ss"] = [mk(ph, "ss%d" % i, [128, 1], F32) for i in range(2)]
                        st["sd"] = [mk(ph, "sd%d" % i, [128, 1], F32) for i in range(2)]
                        st["rs"] = [mk(ph, "rs%d" % i, [128, 1], F32) for i in range(2)]
                        st["pt"] = [mk(ph, "pt%d" % i, [128, KC, 128], BF16, psum=True) for i in range(2)]
                        pG = [mk(ph, "pG%d" % i, [128, 512], F32, psum=True) for i in range(4)]
                        pY = [mk(ph, "pY%d" % i, [128, 512], F32, psum=True) for i in range(2)]
                        ws = [mk(ph, "ws%d" % i, [128, KC, 1024], BF16) for i in range(2)]
                        hblk = mk(ph, "hblk", [128, KC, 344], BF16)
                        gaT = mk(ph, "gaT", [128, KC, 344], BF16)
                        gbT = mk(ph, "gbT", [128, KC, 344], BF16)
                        tmpa = mk(ph, "tmpa", [128, 344], F32)
                        tmpb = mk(ph, "tmpb", [128, 344], F32)
                        mrgb = mk(ph, "mrgb", [128, KC, 344], BF16)
                        x1t = [mk(ph, "x1t%d" % i, [128, D], F32) for i in range(1)]
                        cw = 0
                        cg = 0
                        cx = 0
                        wg3 = wg_bf.rearrange("(c p) n -> p c n", p=128)

                        def ld(dram3):
                            nonlocal cw
                            w = ws[cw % 2]
                            cw += 1
                            kc = dram3.shape[1]
                            S.dma("sp", w[:, 0:kc, :], dram3, w.r, writes=[w.r])
                            return w

                        for (blo, bhi) in split(qa, qb, 344):
                            N = bhi - blo
                            tl = split(blo, bhi, 128)
                            xs_of = {}
                            for (lo, hi) in tl:
                                nt = hi - lo
                                xs = st["xs"][st["xi"] % 3]
                                st["xi"] += 1
                                xs_of[lo] = xs
                                S.dma("sp", xs[0:nt, :], x[u["base"] + lo:u["base"] + hi, :], xs.r, writes=[xs.r])
                                emit_norm(st, xs, nt, 1, 0, hblk[:, :, lo - blo:hi - blo], hblk.r)
                            for gi, (gT, gc0) in enumerate(((gaT, GATE0), (gbT, GATE0 + 1024))):
                                w = ld(wg3[:, :, gc0 - GATE0:gc0 - GATE0 + 1024])
                                for m in range(KC):
                                    p = pG[cg % 4]
                                    cg += 1
                                    S.op("pe", mm_acc(p[:, 0:N], lambda c, w=w, m=m: w[:, c, 128 * m:128 * m + 128],
                                                      lambda c: hblk[:, c, 0:N], KC), reads=[w.r, hblk.r], writes=[p.r])
                                    S.op("act", I("activation", out=gT[:, m, 0:N], in_=p[:, 0:N], func=AF.Sigmoid),
                                         reads=[p.r], writes=[gT.r])
                            wa = ld(pa_bf.rearrange("(c p) n -> p c n", p=128))
                            wb_ = ld(pb_bf.rearrange("(c p) n -> p c n", p=128))
                            for m in range(KC):
                                p = pG[cg % 4]
                                cg += 1
                                p2 = pG[cg % 4]
                                cg += 1
                                S.op("pe", mm_acc(p[:, 0:N], lambda c, m=m: wa[:, c, 128 * m:128 * m + 128],
                                                  lambda c: o_aT[:, c, blo - qa:bhi - qa], 8), reads=[wa.r], writes=[p.r])
                                S.op("pe", mm_acc(p2[:, 0:N], lambda c, m=m: wb_[:, c, 128 * m:128 * m + 128],
                                                  lambda c: o_bT[:, c, blo - qa:bhi - qa], 2), reads=[wb_.r], writes=[p2.r])
                                S.op("dve", I("tensor_tensor", out=tmpa[:, 0:N], in0=p[:, 0:N], in1=gaT[:, m, 0:N], op=ALU.mult),
                                     reads=[p.r, gaT.r], writes=[tmpa.r])
                                S.op("dve", I("tensor_tensor", out=tmpb[:, 0:N], in0=p2[:, 0:N], in1=gbT[:, m, 0:N], op=ALU.mult),
                                     reads=[p2.r, gbT.r], writes=[tmpb.r])
                                S.op("dve", I("tensor_tensor", out=mrgb[:, m, 0:N], in0=tmpa[:, 0:N], in1=tmpb[:, 0:N], op=ALU.add),
                                     reads=[tmpa.r, tmpb.r], writes=[mrgb.r])
                            w = ld(wo_bf.rearrange("(c p) n -> p c n", p=128))
                            for (lo, hi) in tl:
                                nt = hi - lo
                                xs = xs_of[lo]
                                x1 = x1t[0]
                                cx += 1
                                for nh in range(2):
                                    p = pY[nh]
                                    S.op("pe", mm_acc(p[0:nt, :], lambda c, lo=lo, hi=hi: mrgb[:, c, lo - blo:hi - blo],
                                                      lambda c, w=w, nh=nh: w[:, c, 512 * nh:512 * nh + 512], KC),
                                         reads=[w.r, mrgb.r], writes=[p.r])
                                    S.op("dve", I("tensor_tensor", out=x1[0:nt, 512 * nh:512 * nh + 512], in0=p[0:nt, :], in1=modbc[0:nt, 2, 512 * nh:512 * nh + 512], op=ALU.mult),
                                        reads=[p.r, modbc.r], writes=[x1.r])
                                S.op("dve", I("tensor_tensor", out=x1[0:nt, :], in0=x1[0:nt, :], in1=xs[0:nt, :], op=ALU.add),
                                     reads=[x1.r, xs.r], writes=[x1.r])
                                S.dma("pool", x1s[lo - qa:hi - qa, :], x1[0:nt, :], x1.r, reads=[x1.r])
                                emit_norm(st, x1, nt, 4, 3, h2T[:, :, lo - qa:hi - qa], h2T.r)
                        S.run_block()

                with ExitStack() as ph:
                    wdn = mk(ph, "wdn", [128, FC, D], BF16)
                    actT = mk(ph, "actT", [128, FC, 512], BF16)
                    wup = [mk(ph, "wup%d" % i, [128, KC, 256], BF16) for i in range(2)]
                    u_sb = mk(ph, "u_sb", [128, 516], F32)
                    acc = mk(ph, "acc", [128, 512], F32)
                    gl = mk(ph, "gl", [128, 512], F32)
                    x1l = [mk(ph, "x1l%d" % i, [128, D], F32) for i in range(2)]
                    x2 = mk(ph, "x2", [128, D], F32)
                    sq = mk(ph, "sq5", [128, D], F32)
                    ot = mk(ph, "ot", [128, D], F32)
                    ss = mk(ph, "ss5", [128, 1], F32)
                    sd = mk(ph, "sd5", [128, 1], F32)
                    rs = mk(ph, "rs5", [128, 1], F32)
                    pU = [mk(ph, "pU%d" % i, [128, 512], F32, psum=True) for i in range(4)]
                    pV = [mk(ph, "pV%d" % i, [128, 512], F32, psum=True) for i in range(2)]
                    pD = [mk(ph, "pD%d" % i, [128, 512], F32, psum=True) for i in range(2)]
                    S.dma("sp", wdn[:], wdn_bf.rearrange("(c p) n -> p c n", p=128), wdn.r, writes=[wdn.r])
                    cu = 0
                    cwu = 0
                    cxl = 0
                    for (o_lo, o_hi) in split(q0, q1, 512):
                        no = o_hi - o_lo
                        c_lo = max(qa, o_lo - 1)
                        c_hi = min(qb, o_hi + 1)
                        ncol = c_hi - c_lo
                        j0 = o_lo - c_lo
                        pieces = split(0, ncol, (ncol + 1) // 2)
                        for i in range(FC):
                            w = wup[cwu % 2]
                            cwu += 1
                            S.dma("sp", w[:, :, :], wup_arr[i], w.r, writes=[w.r])
                            pus = []
                            for pi, (a_, b_) in enumerate(pieces):
                                pu = pU[cu % 4]
                                cu += 1
                                pus.append(pu)
                                S.op("pe", mm_acc(pu[:, 0:b_ - a_], lambda c, w=w: w[:, c, 0:128],
                                                  lambda c, a_=a_, b_=b_: h2T[:, c, c_lo - qa + a_:c_lo - qa + b_], KC),
                                     reads=[w.r], writes=[pu.r])
                                S.op("act", I("activation", out=u_sb[:, a_:b_], in_=pu[:, 0:b_ - a_], func=AF.Copy),
                                     reads=[pu.r], writes=[u_sb.r])
                            for pi, (a_, b_) in enumerate(pieces):
                                pv_ = pV[pi]
                                S.op("pe", mm_acc(pv_[:, 0:b_ - a_], lambda c, w=w: w[:, c, 128:256],
                                                  lambda c, a_=a_, b_=b_: h2T[:, c, c_lo - qa + a_:c_lo - qa + b_], KC),
                                     reads=[w.r], writes=[pv_.r])
                            S.op("dve", I("tensor_scalar", out=acc[:, 0:no], in0=u_sb[:, j0:j0 + no], scalar1=convw[:, i, 1:2],
                                                                     scalar2=convb[:, i:i + 1], op0=ALU.mult, op1=ALU.add),
                                 reads=[u_sb.r, convw.r, convb.r], writes=[acc.r])
                            if j0 == 1:
                                la, lb, ls = 0, no, 0
                            else:
                                la, lb, ls = 1, no, 0
                            S.op("dve", I("scalar_tensor_tensor", out=acc[:, la:lb], in0=u_sb[:, ls:ls + (lb - la)], scalar=convw[:, i, 0:1], in1=acc[:, la:lb],
                                op0=ALU.mult, op1=ALU.add), reads=[u_sb.r, convw.r, acc.r], writes=[acc.r])
                            if c_hi > o_hi:
                                ra_, rb_ = 0, no
                            else:
                                ra_, rb_ = 0, no - 1
                            S.op("dve", I("scalar_tensor_tensor", out=acc[:, ra_:rb_], in0=u_sb[:, j0 + 1 + ra_:j0 + 1 + rb_], scalar=convw[:, i, 2:3], in1=acc[:, ra_:rb_],
                                op0=ALU.mult, op1=ALU.add), reads=[u_sb.r, convw.r, acc.r], writes=[acc.r])
                            S.op("act", I("activation", out=gl[:, 0:no], in_=acc[:, 0:no], func=AF.Gelu_apprx_tanh),
                                 reads=[acc.r], writes=[gl.r])
                            for pi, (a_, b_) in enumerate(pieces):
                                lo_ = max(a_, j0)
                                hi_ = min(b_, j0 + no)
                                if hi_ <= lo_:
                                    continue
                                pv_ = pV[pi]
                                S.op("dve", I("tensor_tensor", out=actT[:, i, lo_ - j0:hi_ - j0], in0=gl[:, lo_ - j0:hi_ - j0], in1=pv_[:, lo_ - a_:hi_ - a_], op=ALU.mult),
                                    reads=[gl.r, pv_.r], writes=[actT.r])
                        for (lo, hi) in split(o_lo, o_hi, 128):
                            nt = hi - lo
                            xl = x1l[cxl % 2]
                            cxl += 1
                            S.dma("sp", xl[0:nt, :], x1s[lo - qa:hi - qa, :], xl.r, writes=[xl.r])
                            for nh in range(2):
                                p = pD[nh]
                                S.op("pe", mm_acc(p[0:nt, :], lambda c, lo=lo, hi=hi: actT[:, c, lo - o_lo:hi - o_lo],
                                                  lambda c, nh=nh: wdn[:, c, 512 * nh:512 * nh + 512], FC),
                                     reads=[wdn.r, actT.r], writes=[p.r])
                                S.op("dve", I("tensor_tensor", out=x2[0:nt, 512 * nh:512 * nh + 512], in0=p[0:nt, :], in1=modbc[0:nt, 5, 512 * nh:512 * nh + 512], op=ALU.mult),
                                    reads=[p.r, modbc.r], writes=[x2.r])
                            S.op("dve", I("tensor_tensor", out=x2[0:nt, :], in0=x2[0:nt, :], in1=xl[0:nt, :], op=ALU.add),
                                 reads=[x2.r, xl.r], writes=[x2.r])
                            S.op("act", I("activation", out=sq[0:nt, :], in_=x2[0:nt, :], func=AF.Square), reads=[x2.r], writes=[sq.r])
                            S.op("dve", I("tensor_reduce", out=ss[0:nt, 0:1], in_=sq[0:nt, :], axis=AX.X, op=ALU.add),
                                 reads=[sq.r], writes=[ss.r])
                            S.op("act", I("activation", out=sd[0:nt, 0:1], in_=ss[0:nt, 0:1], func=AF.Ln,
                                          bias=eps_t[0:nt, 0:1], scale=1.0 / D), reads=[ss.r, eps_t.r], writes=[sd.r])
                            S.op("act", I("activation", out=rs[0:nt, 0:1], in_=sd[0:nt, 0:1], func=AF.Exp, scale=-0.5), reads=[sd.r], writes=[rs.r])
                            S.op("dve", I("scalar_tensor_tensor", out=ot[0:nt, :], in0=x2[0:nt, :], scalar=rs[0:nt, 0:1],
                                                                                in1=gf_bc[0:nt, :], op0=ALU.mult, op1=ALU.mult),
                                 reads=[x2.r, rs.r, gf_bc.r], writes=[ot.r])
                            yr = u["ybase"] + (lo - q0)
                            S.dma("pool", y[yr:yr + nt, :], ot[0:nt, :], ot.r, reads=[ot.r])
                    S.run_block()
    return nc


FULL_CFG = dict(SA=2048, SB=8192, QB=4096, UQ=1024)


def rope_table(pos):
    inv = (np.float32(10000.0) ** (-(np.arange(0, 32, 2, dtype=np.float32)) / np.float32(32))).astype(np.float32)
    ang = (pos.astype(np.float32)[:, None] * inv[None, :]).astype(np.float32)
    c = np.cos(ang).astype(np.float32).T
    s = np.sin(ang).astype(np.float32).T
    tab = np.stack([np.concatenate([c, c], 0), np.concatenate([s, s], 0)], axis=1)
    return np.ascontiguousarray(tab.astype(np.float32))


def const_tables():
    kk = np.arange(128)[:, None]
    cc = np.arange(256)[None, :]
    rel = kk + 64 - cc
    arel = np.abs(rel).astype(np.float32)
    mband = np.where((cc >= kk) & (cc <= kk + 128), 0.0, NEG).astype(np.float32)
    sel = np.zeros((128, 64), np.float32)
    sel[64, :] = 1.0
    return np.eye(128, dtype=np.float32), sel, arel, mband


def core_inputs(cfg, xa, xb, ca, cb, W, reverse_b):
    SB = cfg["SB"]
    f = lambda a: np.ascontiguousarray(np.asarray(a, dtype=np.float32))
    if reverse_b:
        xb = xb[::-1]
        posb = (SB - 1 - np.arange(SB))
        cwb = W["conv_w"][::-1]
    else:
        posb = np.arange(SB)
        cwb = W["conv_w"]
    ident, sel, arel, mband = const_tables()
    pl = lambda v, k: f(np.asarray(v).reshape(k, 128).T)
    cwl = lambda cw: np.asarray(cw).reshape(3, FC, 128).transpose(2, 1, 0)
    m = {
        "x": f(np.concatenate([xa, xb], 0)),
        "cT": f(np.stack([pl(ca, KC), pl(cb, KC)], 0)),
        "ada_w": f(W["ada_w"]), "ada_b": f(W["ada_b"]), "norm1_g": f(W["norm1_g"]), "w_in": f(W["w_in"]),
        "gq": pl(W["q_norm_g"], 3), "gkv": pl(W["kv_norm_g"], 2), "w_uq": f(W["w_uq"]), "w_ukv": f(W["w_ukv"]),
        "p_a": f(W["p_a"]), "p_b": f(W["p_b"]), "w_out": f(W["w_out"]), "norm2_g": f(W["norm2_g"]),
        "w_up": f(W["w_up"]), "convw": f(np.stack([cwl(W["conv_w"]), cwl(cwb)], 0)), "convb": pl(W["conv_b"], FC),
        "w_down": f(W["w_down"]), "normf_g": f(W["normf_g"]),
        "ropeA": rope_table(np.arange(cfg["SA"])), "ropeB": rope_table(posb),
        "ident": ident, "sel": sel, "arel": arel, "mband": mband,
    }
    return m


_NC_CACHE = {}


def kernel(x_prompt, x_sample, c_prompt, c_sample, ada_w, ada_b, norm1_g, w_in, q_norm_g, kv_norm_g,
           w_uq, w_ukv, p_a, p_b, w_out, norm2_g, w_up, conv_w, conv_b, w_down, normf_g):
    cfg = FULL_CFG
    W = dict(ada_w=np.asarray(ada_w)[0], ada_b=np.asarray(ada_b)[0], norm1_g=np.asarray(norm1_g)[0], w_in=np.asarray(w_in)[0],
             q_norm_g=np.asarray(q_norm_g)[0], kv_norm_g=np.asarray(kv_norm_g)[0], w_uq=np.asarray(w_uq)[0],
             w_ukv=np.asarray(w_ukv)[0], p_a=np.asarray(p_a)[0], p_b=np.asarray(p_b)[0], w_out=np.asarray(w_out)[0],
             norm2_g=np.asarray(norm2_g)[0], w_up=np.asarray(w_up)[0], conv_w=np.asarray(conv_w)[0],
             conv_b=np.asarray(conv_b)[0], w_down=np.asarray(w_down)[0], normf_g=np.asarray(normf_g))
    x_prompt = np.asarray(x_prompt)
    x_sample = np.asarray(x_sample)
    c_prompt = np.asarray(c_prompt)
    c_sample = np.asarray(c_sample)
    in_maps = []
    for i in range(8):
        in_maps.append(core_inputs(cfg, x_prompt[i], x_sample[i // 2], c_prompt[i], c_sample[i // 2], W, reverse_b=(i % 2 == 1)))
    nc = build(cfg)
    res = run_bass_kernel_spmd(nc, in_maps, core_ids=list(range(8)))
    SA, QB = cfg["SA"], cfg["QB"]
    y_prompt = np.empty((8, SA, D), np.float32)
    y_sample = np.empty((4, cfg["SB"], D), np.float32)
    for i in range(8):
        yy = np.asarray(res.results[i]["y"])
        y_prompt[i] = yy[0:SA]
        if i % 2 == 0:
            y_sample[i // 2, 0:QB] = yy[SA:SA + QB]
        else:
            y_sample[i // 2, cfg["SB"] - QB:] = yy[SA:SA + QB][::-1]
    return (y_prompt, y_sample)
```

```python
import math
from contextlib import ExitStack
import numpy as np
import concourse.bass as bass
import concourse.mybir as mybir
from concourse.bass_utils import run_bass_kernel_spmd

F32 = mybir.dt.float32
BF16 = mybir.dt.bfloat16
AF = mybir.ActivationFunctionType
ALU = mybir.AluOpType
AX = mybir.AxisListType

D = 1024
KC = 8
DFF = 2816
FC = 22
EPS = 1e-6
NEG = -30000.0
DILS = (1, 4, 16)
MLA_SCALE = 96.0 ** -0.5
IN_COLS = 5024
GATE0 = 2976
DQ0 = 672


def I(name, *args, **kw):
    return lambda e: getattr(e, name)(*args, **kw)


def split(a, b, m):
    n = b - a
    if n <= 0:
        return []
    k = (n + m - 1) // m
    base, rem = divmod(n, k)
    out = []
    lo = a
    for i in range(k):
        sz = base + (1 if i < rem else 0)
        out.append((lo, lo + sz))
        lo += sz
    return out


class Res:
    __slots__ = ("name", "w", "r", "dsem", "dcnt")

    def __init__(self, name):
        self.name = name
        self.w = None
        self.r = []
        self.dsem = None
        self.dcnt = 0


class Op:
    __slots__ = ("eng", "fn", "waits", "signal", "val", "sem", "isdma")


class Sched:
    ATTR = {"pe": "tensor", "act": "scalar", "dve": "vector", "pool": "gpsimd", "sp": "sync"}

    def __init__(self, nc, stack):
        self.nc = nc
        self.stack = stack
        self.esem = {e: stack.enter_context(nc.semaphore("sem_" + e)) for e in self.ATTR}
        self.ecount = {e: 0 for e in self.ATTR}
        self.ops = {e: [] for e in self.ATTR}
        self.res_all = []
        self.nops = 0
        self.dpool = []
        self.dused = []
        self.nds = 0

    def res(self, name="r"):
        r = Res(name)
        self.res_all.append(r)
        return r

    def _record(self, op, reads, writes):
        deps = []
        for r in reads:
            if r.w is not None:
                deps.append((r.w, True))
        for r in writes:
            if r.w is not None:
                deps.append((r.w, False))
            for x in r.r:
                deps.append((x, False))
        for d, raw in deps:
            if d is op:
                continue
            if d.eng == op.eng and not d.isdma and not op.isdma:
                if op.eng == "pe":
                    continue
                if not raw:
                    continue
            if d not in op.waits:
                if not d.isdma:
                    d.signal = True
                op.waits.append(d)
        for r in reads:
            r.r.append(op)
        for r in writes:
            r.w = op
            r.r = []
        self.ops[op.eng].append(op)
        self.nops += 1

    def op(self, eng, fn, reads=(), writes=()):
        o = Op()
        o.eng = eng
        o.fn = fn
        o.waits = []
        o.signal = False
        o.val = None
        o.sem = None
        o.isdma = False
        self._record(o, reads, writes)
        return o

    def dma(self, eng, out, in_, semres, reads=(), writes=(), **kw):
        o = Op()
        o.eng = eng
        o.waits = []
        o.signal = True
        o.isdma = True
        if semres.dsem is None:
            if self.dpool:
                semres.dsem = self.dpool.pop()
            else:
                self.nds += 1
                semres.dsem = [self.stack.enter_context(self.nc.semaphore("dsem_%d" % self.nds)), 0]
            self.dused.append(semres)
        semres.dsem[1] += 16
        o.sem = semres.dsem[0]
        o.val = semres.dsem[1]
        o.fn = I("dma_start", out=out, in_=in_, **kw)
        self._record(o, reads, writes)
        return o

    def run_block(self):
        for e, lst in self.ops.items():
            for o in lst:
                if not o.isdma and o.signal:
                    self.ecount[e] += 1
                    o.val = self.ecount[e]
                    o.sem = self.esem[e]
        with self.nc.Block(no_gpsimd_drain=True) as blk:
            for e in self.ATTR:
                lst = self.ops[e]
                if not lst:
                    continue

                def body(eng, lst=lst):
                    waited = {}
                    finals = {}
                    for o in lst:
                        for d in o.waits:
                            k = id(d.sem)
                            if waited.get(k, -1) >= d.val:
                                continue
                            eng.wait_ge(d.sem, d.val)
                            waited[k] = d.val
                        ins = o.fn(eng)
                        if o.isdma:
                            ins.then_inc(o.sem, 16)
                            finals[id(o.sem)] = (o.sem, o.val)
                        elif o.signal:
                            ins.then_inc(o.sem, 1)
                    for sem, val in finals.values():
                        eng.wait_ge(sem, val)

                getattr(blk, self.ATTR[e])(body)
        self.ops = {e: [] for e in self.ATTR}
        for r in self.res_all:
            r.w = None
            r.r = []
        for r in self.dused:
            self.dpool.append(r.dsem)
            r.dsem = None
        self.dused = []


class T:
    CNT = [0]

    def __init__(self, S, stack, nc, name, shape, dtype, psum=False):
        T.CNT[0] += 1
        name = "%s_%d" % (name, T.CNT[0])
        if psum:
            self.t = stack.enter_context(nc.psum_tensor(name, shape, dtype))
        else:
            self.t = stack.enter_context(nc.sbuf_tensor(name, shape, dtype))
        self.r = S.res(name)

    def __getitem__(self, k):
        return self.t[k]


def make_units(cfg):
    SA, SB, QB, UQ = cfg["SA"], cfg["SB"], cfg["QB"], cfg["UQ"]
    units = []
    for seq, base, Sk, Qn, yb in ((0, 0, SA, SA, 0), (1, SA, SB, QB, SA)):
        for q0 in range(0, Qn, UQ):
            qa = max(0, q0 - 1)
            qb = min(Sk, q0 + UQ + 1)
            ra = max(0, qa - 1024)
            rb = min(Sk, qb + 1024)
            units.append(dict(seq=seq, base=base, Sk=Sk, q0=q0, q1=q0 + UQ, qa=qa, qb=qb, ra=ra, rb=rb,
                              first=(q0 == 0), ybase=yb + q0))
    return units


def build(cfg):
    SA, SB, QB, UQ = cfg["SA"], cfg["SB"], cfg["QB"], cfg["UQ"]
    NTOK = SA + SB
    NOUT = SA + QB
    SKMAX = max(SA, SB)
    NQM = UQ + 2
    NRM = UQ + 2 + 2048
    nc = bass.Bass("TRN2", target_bir_lowering=False)

    def din(name, shape, dt=F32):
        return nc.dram_tensor(name, list(shape), dt, kind="ExternalInput").ap()

    x = din("x", [NTOK, D])
    cT_d = din("cT", [2, 128, KC])
    ada_w = din("ada_w", [D, 6 * D])
    ada_b = din("ada_b", [6 * D])
    g1_d = din("norm1_g", [D])
    w_in = din("w_in", [D, IN_COLS])
    gq_d = din("gq", [128, 3])
    gkv_d = din("gkv", [128, 2])
    w_uq = din("w_uq", [384, 1536])
    w_ukv = din("w_ukv", [256, 2048])
    p_a = din("p_a", [1024, 1024])
    p_b = din("p_b", [256, 1024])
    w_out = din("w_out", [1024, 1024])
    g2_d = din("norm2_g", [D])
    w_up = din("w_up", [D, 2 * DFF])
    convw_d = din("convw", [2, 128, FC, 3])
    convb_d = din("convb", [128, FC])
    w_down = din("w_down", [DFF, D])
    gf_d = din("normf_g", [D])
    ropeA = din("ropeA", [32, 2, SA])
    ropeB = din("ropeB", [32, 2, SB])
    ident_d = din("ident", [128, 128])
    sel_d = din("sel", [128, 64])
    arel_d = din("arel", [128, 256])
    mband_d = din("mband", [128, 256])
    y = nc.dram_tensor("y", [NOUT, D], F32, kind="ExternalOutput").ap()
    x1s = nc.dram_tensor("x1s", [NQM, D], F32, kind="Internal").ap()
    wg_bf = nc.dram_tensor("wg_bf", [D, 2048], BF16, kind="Internal").ap()
    kscr = nc.dram_tensor("kscr", [16, 64, SKMAX], BF16, kind="Internal").ap()
    vscr = nc.dram_tensor("vscr", [16, 128, (SKMAX // 128) * 65], BF16, kind="Internal").ap()
    pa_bf = nc.dram_tensor("pa_bf", [1024, 1024], BF16, kind="Internal").ap()
    pb_bf = nc.dram_tensor("pb_bf", [256, 1024], BF16, kind="Internal").ap()
    wo_bf = nc.dram_tensor("wo_bf", [1024, 1024], BF16, kind="Internal").ap()
    wup_arr = nc.dram_tensor("wup_arr", [FC, 128, KC, 256], BF16, kind="Internal").ap()
    wdn_bf = nc.dram_tensor("wdn_bf", [DFF, D], BF16, kind="Internal").ap()
    ropes = (ropeA, ropeB)

    units = make_units(cfg)

    with ExitStack() as top:
        S = Sched(nc, top)

        def mk(stack, name, shape, dt, psum=False):
            return T(S, stack, nc, name, shape, dt, psum)

        ident = mk(top, "ident", [128, 128], BF16)
        sel = mk(top, "sel", [128, 64], F32)
        arel = mk(top, "arel", [128, 256], F32)
        mband = mk(top, "mband", [128, 256], F32)
        eps_t = mk(top, "eps_t", [128, 1], F32)
        ones_bf = mk(top, "ones_bf", [128, 128], BF16)
        ones_f = mk(top, "ones_f", [128, 128], F32)
        gf_bc = mk(top, "gf_bc", [128, D], F32)
        gq = mk(top, "gq", [128, 3], F32)
        gkv = mk(top, "gkv", [128, 2], F32)
        convw = mk(top, "convw", [128, FC, 3], F32)
        convb = mk(top, "convb", [128, FC], F32)
        modbc = mk(top, "modbc", [128, 6, D], F32)
        ckvnT = mk(top, "ckvnT", [128, 2, SKMAX], BF16)
        KT = mk(top, "KT", [96, SKMAX], BF16)

        S.dma("pool", ident[:], ident_d[:, :], ident.r, writes=[ident.r])
        S.dma("sp", sel[:], sel_d[:, :], sel.r, writes=[sel.r])
        S.dma("sp", arel[:], arel_d[:, :], arel.r, writes=[arel.r])
        S.dma("sp", mband[:], mband_d[:, :], mband.r, writes=[mband.r])
        S.dma("sp", gq[:], gq_d[:, :], gq.r, writes=[gq.r])
        S.dma("sp", gkv[:], gkv_d[:, :], gkv.r, writes=[gkv.r])
        S.dma("sp", convb[:], convb_d[:, :], convb.r, writes=[convb.r])
        S.dma("sp", gf_bc[:], gf_d.partition_broadcast(128), gf_bc.r, writes=[gf_bc.r])
        S.op("dve", I("memset", eps_t[:], EPS), writes=[eps_t.r])
        S.op("dve", I("memset", ones_bf[:], 1.0), writes=[ones_bf.r])
        S.op("dve", I("memset", ones_f[:], 1.0), writes=[ones_f.r])
        S.run_block()

        def emit_hT(st, u, tiles, Gi, SHi, dst_fn, dres, tag):
            ctxs = []
            for i, (lo, hi) in enumerate(tiles):
                nt = hi - lo
                xs = st["xs"][st["xi"] % len(st["xs"])]
                st["xi"] += 1
                S.dma("sp", xs[0:nt, :], x[u["base"] + lo:u["base"] + hi, :], xs.r, writes=[xs.r])
                ctxs.append(emit_normA(st, xs, nt))
                if i >= 1:
                    plo, phi = tiles[i - 1]
                    emit_normB(st, ctxs[i - 1], Gi, SHi, dst_fn(plo, phi), dres)
            if tiles:
                plo, phi = tiles[-1]
                emit_normB(st, ctxs[-1], Gi, SHi, dst_fn(plo, phi), dres)

        def emit_normA(st, xs, nt):
            k_ = st["pti"] % 2
            st["pti"] += 1
            hn, hb = st["hn"][k_], st["hb"][k_]
            ss, sd, rs = st["ss"][k_], st["sd"][k_], st["rs"][k_]
            sq = hn
            pt = st["pt"][k_ % len(st["pt"])]
            S.op("act", I("activation", out=sq[0:nt, :], in_=xs[0:nt, :], func=AF.Square, accum_out=ss[0:nt, 0:1]),
                 reads=[xs.r], writes=[sq.r, ss.r])
            S.op("act", I("activation", out=sd[0:nt, 0:1], in_=ss[0:nt, 0:1], func=AF.Ln,
                          bias=eps_t[0:nt, 0:1], scale=1.0 / D),
                 reads=[ss.r, eps_t.r], writes=[sd.r])
            S.op("act", I("activation", out=rs[0:nt, 0:1], in_=sd[0:nt, 0:1], func=AF.Exp, scale=-0.5), reads=[sd.r], writes=[rs.r])
            return (xs, nt, hn, hb, rs, pt)

        def emit_normB(st, ctx, Gi, SHi, dst, dres):
            xs, nt, hn, hb, rs, pt = ctx
            S.op("dve", I("scalar_tensor_tensor", out=hn[0:nt, :], in0=xs[0:nt, :], scalar=rs[0:nt, 0:1],
                          in1=modbc[0:nt, Gi, :], op0=ALU.mult, op1=ALU.mult),
                 reads=[xs.r, rs.r, modbc.r], writes=[hn.r])
            S.op("dve", I("tensor_tensor", out=hb[0:nt, :], in0=hn[0:nt, :], in1=modbc[0:nt, SHi, :],
                          op=ALU.add),
                 reads=[hn.r, modbc.r], writes=[hb.r])

            def tr(e):
                ins = None
                for c in range(KC):
                    ins = e.transpose(out=pt[:, c, 0:nt], in_=hb[0:nt, c * 128:(c + 1) * 128],
                                      identity=ident[0:nt, 0:nt])
                return ins

            S.op("pe", tr, reads=[hb.r, ident.r], writes=[pt.r])
            S.op("act", I("activation", out=dst, in_=pt[:, :, 0:nt], func=AF.Copy),
                 reads=[pt.r], writes=[dres])

        def emit_norm(st, xs, nt, Gi, SHi, dst, dres):
            emit_normB(st, emit_normA(st, xs, nt), Gi, SHi, dst, dres)

        def mm_acc(ps_ap, lhs_fn, rhs_fn, nk):
            pairs = [(lhs_fn(c), rhs_fn(c)) for c in range(nk)]

            def f(e):
                ins = None
                for c, (l_, r_) in enumerate(pairs):
                    ins = e.matmul(ps_ap, l_, r_, start=(c == 0), stop=(c == nk - 1))
                return ins
            return f

        def wload(wt, dram_ap, eng="pool"):
            S.dma(eng, wt, dram_ap.rearrange("(c p) n -> p c n", p=128), wt_res[0], writes=[wt_res[0]])

        wt_res = [None]

        cur_seq = -1
        for u in units:
            seq = u["seq"]
            Sk = u["Sk"]
            qa, qb, ra, rb, q0, q1 = u["qa"], u["qb"], u["ra"], u["rb"], u["q0"], u["q1"]
            nq = qb - qa
            nR = rb - ra
            rope = ropes[seq]

            if seq != cur_seq:
                cur_seq = seq
                with ExitStack() as ph:
                    cTt = mk(ph, "cTt", [128, KC], F32)
                    sg = mk(ph, "sg", [128, KC], F32)
                    sv = mk(ph, "sv", [128, KC], F32)
                    srep = mk(ph, "srep", [128, KC, 128], BF16)
                    aw = [mk(ph, "aw%d" % i, [128, KC, 512], BF16) for i in range(3)]
                    ab = [mk(ph, "ab%d" % i, [128, 512], F32) for i in range(2)]
                    gbc = [mk(ph, "gbc%d" % i, [128, D], F32) for i in range(2)]
                    pm = [mk(ph, "pm%d" % i, [128, 512], F32, psum=True) for i in range(2)]
                    S.dma("sp", cTt[:], cT_d[seq], cTt.r, writes=[cTt.r])
                    S.dma("sp", convw[:], convw_d[seq], convw.r, writes=[convw.r])
                    S.dma("sp", gbc[0][:], g1_d.partition_broadcast(128), gbc[0].r, writes=[gbc[0].r])
                    S.dma("sp", gbc[1][:], g2_d.partition_broadcast(128), gbc[1].r, writes=[gbc[1].r])
                    S.op("act", I("activation", out=sg[:], in_=cTt[:], func=AF.Sigmoid), reads=[cTt.r], writes=[sg.r])
                    S.op("dve", I("tensor_tensor", out=sv[:], in0=cTt[:], in1=sg[:], op=ALU.mult),
                         reads=[cTt.r, sg.r], writes=[sv.r])
                    for c in range(KC):
                        S.op("dve", I("tensor_scalar", out=srep[:, c, :], in0=ones_bf[:], scalar1=sv[:, c:c + 1],
                                                                   scalar2=None, op0=ALU.mult),
                             reads=[sv.r, ones_bf.r], writes=[srep.r])
                    for j in range(12):
                        a = aw[j % 3]
                        b = ab[j % 2]
                        p = pm[j % 2]
                        S.dma("pool", a[:], ada_w[:, j * 512:(j + 1) * 512].rearrange("(c p) n -> p c n", p=128), a.r, writes=[a.r])
                        S.dma("sp", b[:], ada_b[j * 512:(j + 1) * 512].partition_broadcast(128), b.r, writes=[b.r])
                        S.op("pe", mm_acc(p[:, :], lambda c: srep[:, c, :], lambda c, a=a: a[:, c, :], KC),
                             reads=[srep.r, a.r], writes=[p.r])
                        S.op("dve", I("tensor_tensor", out=modbc[:, j // 2, (j % 2) * 512:(j % 2) * 512 + 512], in0=p[:, :], in1=b[:, :], op=ALU.add),
                            reads=[p.r, b.r], writes=[modbc.r])
                    for gi, mi in ((0, 1), (1, 4)):
                        S.op("dve", I("scalar_tensor_tensor", out=modbc[:, mi, :], in0=modbc[:, mi, :], scalar=1.0, in1=gbc[gi][:, :],
                            op0=ALU.add, op1=ALU.mult), reads=[modbc.r, gbc[gi].r], writes=[modbc.r])
                    S.run_block()

            with ExitStack() as ust:
                p14 = ust.enter_context(ExitStack())
                cqnT = mk(p14, "cqnT", [128, 3, NQM], BF16)
                o_bT = mk(p14, "o_bT", [128, 2, NQM], BF16)

                with ExitStack() as p12:
                    hT_u = mk(p12, "hT_u", [128, KC, nR], BF16)
                    with ExitStack() as ph:
                        st = dict(xi=0, pti=0)
                        st["xs"] = [mk(ph, "xs%d" % i, [128, D], F32) for i in range(6 if (nR <= 2049 and not (u["first"] and rb < Sk)) else 5)]
                        st["hn"] = [mk(ph, "hn%d" % i, [128, D], F32) for i in range(2)]
                        st["hb"] = [mk(ph, "hb%d" % i, [128, D], BF16) for i in range(2)]
                        st["ss"] = [mk(ph, "ss%d" % i, [128, 1], F32) for i in range(2)]
                        st["sd"] = [mk(ph, "sd%d" % i, [128, 1], F32) for i in range(2)]
                        st["rs"] = [mk(ph, "rs%d" % i, [128, 1], F32) for i in range(2)]
                        st["pt"] = [mk(ph, "pt%d" % i, [128, KC, 128], BF16, psum=True) for i in range(2)]
                        psA = [mk(ph, "psA%d" % i, [128, 512], F32, psum=True) for i in range(3)]
                        psS = mk(ph, "psS", [128, 512], F32, psum=True)
                        psK = mk(ph, "psK", [128, 512], F32, psum=True)
                        psR = mk(ph, "psR", [128, 512], F32, psum=True)
                        wlat = mk(ph, "wlat", [128, KC, 704], BF16)
                        htmp = [mk(ph, "htmp%d" % i, [128, KC, 512], BF16) for i in range(2 if (u["first"] and rb < Sk) else 0)]
                        sqf = [mk(ph, "sqf%d" % i, [128, 512], F32) for i in range(3)]
                        sdt = sqf[2]
                        rsb = mk(ph, "rsb", [128, 512], F32)
                        csk = [mk(ph, "csk%d" % i, [32, 2, 512], F32) for i in range(1)]
                        t1 = rsb
                        t2 = sdt
                        hres = {}

                        win3 = w_in.rearrange("(c p) n -> p c n", p=128)
                        S.dma("pool", wlat[:, :, 0:672], win3[:, :, 0:672], wlat.r, writes=[wlat.r])
                        S.dma("pool", wlat[:, :, 672:688], win3[:, :, 656:672], wlat.r, writes=[wlat.r])
                        S.dma("pool", wlat[:, :, 688:704], win3[:, :, 640:656], wlat.r, writes=[wlat.r])
                        S.op("dve", I("tensor_scalar", out=wlat[:, :, 672:688], in0=wlat[:, :, 672:688], scalar1=-1.0,
                                                              scalar2=None, op0=ALU.mult), reads=[wlat.r], writes=[wlat.r])

                        segs = []
                        if u["first"]:
                            cuts = sorted(set([0, ra, qa, qb, rb, Sk]))
                        else:
                            cuts = sorted(set([ra, qa, qb, rb]))
                        for a_, b_ in zip(cuts[:-1], cuts[1:]):
                            segs.append((a_, b_))
                        def lat_parts(blo, bhi, N, hr, hblk, inQ):
                            def rhsf(c):
                                return hblk[:, c, :]

                            def stat_mm(nch, col0):
                                for m in range(nch):
                                    S.op("pe", mm_acc(psA[m][:, 0:N], lambda c, m=m: wlat[:, c, col0 + 128 * m:col0 + 128 * m + 128],
                                                      rhsf, KC), reads=[hr, wlat.r], writes=[psA[m].r])

                            def stat_rest(nch, gt, outT, ocol, ores):
                                for m in range(nch):
                                    S.op("act", I("activation", out=sqf[m][:, 0:N], in_=psA[m][:, 0:N], func=AF.Square),
                                         reads=[psA[m].r], writes=[sqf[m].r])
                                S.op("pe", mm_acc(psS[:, 0:N], lambda c: ones_f[:, :], lambda c: sqf[c][:, 0:N], nch),
                                     reads=[ones_f.r] + [sqf[m].r for m in range(nch)], writes=[psS.r])
                                S.op("act", I("activation", out=sdt[:, 0:N], in_=psS[:, 0:N], func=AF.Ln,
                                              bias=eps_t[:, 0:1], scale=1.0 / (128 * nch)),
                                     reads=[psS.r, eps_t.r], writes=[sdt.r])
                                S.op("act", I("activation", out=rsb[:, 0:N], in_=sdt[:, 0:N], func=AF.Exp, scale=-0.5),
                                     reads=[sdt.r], writes=[rsb.r])
                                for m in range(nch):
                                    S.op("dve", I("scalar_tensor_tensor", out=outT[:, m, ocol:ocol + N], in0=psA[m][:, 0:N], scalar=gt[:, m:m + 1],
                                                  in1=rsb[:, 0:N], op0=ALU.mult, op1=ALU.mult),
                                         reads=[psA[m].r, gt.r, rsb.r], writes=[ores])

                            def P():
                                if u["first"]:
                                    stat_mm(2, 384)
                                    S.op("pe", mm_acc(psK[0:32, 0:N], lambda c: wlat[:, c, 640:672], rhsf, KC),
                                         reads=[hr, wlat.r], writes=[psK.r])
                                    S.op("pe", mm_acc(psR[0:32, 0:N], lambda c: wlat[:, c, 672:704], rhsf, KC),
                                         reads=[hr, wlat.r], writes=[psR.r])

                            def C():
                                if u["first"]:
                                    stat_rest(2, gkv, ckvnT, blo, S.res("ckv"))
                                    ck = csk[0]
                                    S.dma("pool", ck[:, :, 0:N], rope[:, :, blo:bhi], ck.r, writes=[ck.r])
                                    S.op("dve", I("tensor_tensor", out=t1[0:32, 0:N], in0=psR[0:32, 0:N], in1=ck[:, 1, 0:N], op=ALU.mult),
                                         reads=[psR.r, ck.r], writes=[t1.r])
                                    S.op("dve", I("tensor_tensor", out=t2[0:32, 0:N], in0=psK[0:32, 0:N], in1=ck[:, 0, 0:N], op=ALU.mult),
                                         reads=[psK.r, ck.r], writes=[t2.r])
                                    S.op("dve", I("tensor_tensor", out=KT[64:96, blo:bhi], in0=t1[0:32, 0:N], in1=t2[0:32, 0:N], op=ALU.add),
                                         reads=[t1.r, t2.r], writes=[S.res("ktr")])
                                if inQ:
                                    stat_mm(3, 0)
                                    stat_rest(3, gq, cqnT, blo - qa, S.res("cq"))
                            return P, C

                        bi = 0
                        deferred = []
                        for (sa_, sb_) in segs:
                            inR = (sa_ >= ra and sb_ <= rb)
                            inQ = (sa_ >= qa and sb_ <= qb)
                            for (blo, bhi) in split(sa_, sb_, 512):
                                N = bhi - blo
                                if inR:
                                    hr = S.res("hres")
                                    dstf = lambda lo, hi: hT_u[:, :, lo - ra:hi - ra]
                                    hblk = hT_u[:, :, blo - ra:bhi - ra]
                                else:
                                    hb_ = htmp[bi % len(htmp)]
                                    hr = hb_.r
                                    dstf = lambda lo, hi, hb_=hb_, blo=blo: hb_[:, :, lo - blo:hi - blo]
                                    hblk = hb_[:, :, 0:N]
                                bi += 1
                                emit_hT(st, u, split(blo, bhi, 128), 1, 0, dstf, hr, "p1")
                                if u["first"] or inQ:
                                    P_, C_ = lat_parts(blo, bhi, N, hr, hblk, inQ)
                                    while deferred:
                                        deferred.pop(0)()
                                    P_()
                                    deferred.append(C_)
                        while deferred:
                            deferred.pop(0)()
                        S.run_block()

                    with ExitStack() as ph:
                        wd = [mk(ph, "wd%d" % i, [128, KC, 384], BF16) for i in range(2)]
                        qTs_ = [mk(ph, "qTd%d" % i, [128, NQM], BF16) for i in range(2)]
                        kTs_ = [mk(ph, "kTd%d" % i, [128, nR], BF16) for i in range(2)]
                        NTV = 32
                        Vds_ = [mk(ph, "Vd%d" % i, [128, NTV, 2, 65], BF16) for i in range(2)]
                        Uacc = mk(ph, "Uacc", [65, 2, NQM], F32)
                        biasT = [mk(ph, "biasT%d" % i, [128, 256], F32) for i in range(2)]
                        sbt = [mk(ph, "sbt%d" % i, [128, 256], F32) for i in range(3)]
                        Pt = [mk(ph, "Ptd%d" % i, [128, 256], BF16) for i in range(3)]
                        pj = [mk(ph, "pj%d" % i, [128, 512], F32, psum=True) for i in range(2)]
                        pS = [mk(ph, "pS%d" % i, [128, 512], F32, psum=True) for i in range(3)]
                        pO = [mk(ph, "pO%d" % i, [128, 512], F32, psum=True) for i in range(2)]
                        pL = mk(ph, "pL", [128, 512], F32, psum=True)
                        win3 = w_in.rearrange("(c p) n -> p c n", p=128)
                        for Vd in Vds_:
                            S.op("dve", I("memset", Vd[:], 1.0), writes=[Vd.r])
                        cnt_pj = 0
                        cnt_s = 0
                        cnt_o = 0
                        cnt_b = 0
                        cnt_w = 0

                        def pair_task(pp, g):
                            nonlocal cnt_pj, cnt_s, cnt_o, cnt_b, cnt_w
                            if True:
                                d = DILS[g]
                                hd0 = 4 * g + 2 * pp
                                w = wd[cnt_w % 2]
                                qT, kT, Vd = qTs_[cnt_w % 2], kTs_[cnt_w % 2], Vds_[cnt_w % 2]
                                cnt_w += 1
                                for k3 in range(3):
                                    c0 = DQ0 + 768 * k3 + 64 * hd0
                                    S.dma("pool", w[:, :, 128 * k3:128 * k3 + 128], win3[:, :, c0:c0 + 128], w.r, writes=[w.r])
                                for (blo, bhi) in split(qa, qb, 512):
                                    N = bhi - blo
                                    p = pj[cnt_pj % 2]
                                    cnt_pj += 1
                                    S.op("pe", mm_acc(p[:, 0:N], lambda c, w=w: w[:, c, 0:128],
                                                      lambda c, blo=blo, bhi=bhi: hT_u[:, c, blo - ra:bhi - ra], KC),
                                         reads=[w.r], writes=[p.r])
                                    S.op("act", I("activation", out=qT[:, blo - qa:bhi - qa], in_=p[:, 0:N], func=AF.Copy, scale=0.125),
                                        reads=[p.r], writes=[qT.r])
                                for (blo, bhi) in split(ra, rb, 512):
                                    N = bhi - blo
                                    p = pj[cnt_pj % 2]
                                    cnt_pj += 1
                                    S.op("pe", mm_acc(p[:, 0:N], lambda c, w=w: w[:, c, 128:256],
                                                      lambda c, blo=blo, bhi=bhi: hT_u[:, c, blo - ra:bhi - ra], KC),
                                         reads=[w.r], writes=[p.r])
                                    S.op("dve", I("tensor_copy", out=kT[:, blo - ra:bhi - ra], in_=p[:, 0:N]), reads=[p.r], writes=[kT.r])
                                tiles = []
                                for r in range(d):
                                    L = (Sk - r + d - 1) // d
                                    nql = max(0, -((r - qa) // d))
                                    nqh = min(L, -((r - qb) // d))
                                    if nqh <= nql:
                                        continue
                                    klo = max(0, nql - 64)
                                    khi = min(L, nqh + 64)
                                    for (ks, ke) in [(s_, min(s_ + 128, khi)) for s_ in range(klo, khi, 128)]:
                                        ql = max(ks - 64, nql)
                                        qh = min(ke + 64, nqh)
                                        if qh <= ql:
                                            continue
                                        tiles.append((r, ks, ke - ks, ql, qh - ql))
                                assert len(tiles) <= NTV, len(tiles)
                                for ti, (r, ks, cnt, ql, nqq) in enumerate(tiles):
                                    p = pj[cnt_pj % 2]
                                    cnt_pj += 1
                                    tk0 = r + d * ks - ra
                                    S.op("pe", mm_acc(p[0:cnt, 0:128],
                                                      lambda c, tk0=tk0, cnt=cnt, d=d: hT_u[:, c, tk0:tk0 + d * (cnt - 1) + 1:d],
                                                      lambda c, w=w: w[:, c, 256:384], KC),
                                         reads=[w.r], writes=[p.r])
                                    S.op("act", I("activation", out=Vd[0:cnt, ti, :, 0:64], in_=p[0:cnt, 0:128].rearrange("p (a b) -> p a b", a=2),
                                        func=AF.Copy), reads=[p.r], writes=[Vd.r])
                                yield
                                for e_ in range(2):
                                    hd = hd0 + e_
                                    slope = 2.0 ** (-8.0 * (hd + 1) / 12.0)
                                    bt = biasT[cnt_b % 2]
                                    cnt_b += 1
                                    S.op("dve", I("scalar_tensor_tensor", out=bt[:, :], in0=arel[:, :], scalar=-slope * d, in1=mband[:, :], op0=ALU.mult, op1=ALU.add),
                                        reads=[arel.r, mband.r], writes=[bt.r])
                                    pb = 64 * e_
                                    pendd = []

                                    def st2(args):
                                        (po, P_, ti, cnt, nqq, qsl) = args
                                        S.op("pe", I("matmul", po[0:65, 0:nqq], Vd[0:cnt, ti, e_, 0:65], P_[0:cnt, 0:nqq], start=True, stop=True),
                                             reads=[Vd.r, P_.r], writes=[po.r])
                                        S.op("dve", I("tensor_tensor", out=Uacc[0:65, e_, qsl], in0=Uacc[0:65, e_, qsl], in1=po[0:65, 0:nqq], op=ALU.add),
                                             reads=[po.r, Uacc.r], writes=[Uacc.r])

                                    for ti, (r, ks, cnt, ql, nqq) in enumerate(tiles):
                                        ps_ = pS[cnt_s % 3]
                                        sb_ = sbt[cnt_s % 3]
                                        P_ = Pt[cnt_s % 3]
                                        cnt_s += 1
                                        po = pO[cnt_o % 2]
                                        cnt_o += 1
                                        tk0 = r + d * ks - ra
                                        tq0 = r + d * ql - qa
                                        c0 = 64 - (ks - ql)
                                        ksl = slice(tk0, tk0 + d * (cnt - 1) + 1, d)
                                        qsl = slice(tq0, tq0 + d * (nqq - 1) + 1, d)
                                        S.op("pe", I("matmul", ps_[0:cnt, 0:nqq], kT[pb:pb + 64, ksl], qT[pb:pb + 64, qsl], start=True, stop=True),
                                            reads=[kT.r, qT.r], writes=[ps_.r])
                                        S.op("dve", I("tensor_tensor", out=sb_[0:cnt, 0:nqq], in0=ps_[0:cnt, 0:nqq], in1=bt[0:cnt, c0:c0 + nqq], op=ALU.add),
                                            reads=[ps_.r, bt.r], writes=[sb_.r])
                                        S.op("act", I("activation", out=P_[0:cnt, 0:nqq], in_=sb_[0:cnt, 0:nqq], func=AF.Exp),
                                            reads=[sb_.r], writes=[P_.r])
                                        pendd.append((po, P_, ti, cnt, nqq, qsl))
                                        if len(pendd) > 2:
                                            st2(pendd.pop(0))
                                    while pendd:
                                        st2(pendd.pop(0))
                                    if e_ == 0:
                                        yield
                        def normalize_pp(pp):
                            for e_ in range(2):
                                S.op("dve", I("reciprocal", out=Uacc[64:65, e_, 0:nq], in_=Uacc[64:65, e_, 0:nq]),
                                     reads=[Uacc.r], writes=[Uacc.r])
                                for (blo, bhi) in split(0, nq, 512):
                                    N = bhi - blo
                                    S.op("pe", I("matmul", pL[0:64, 0:N], sel[0:65, 0:64], Uacc[0:65, e_, blo:bhi], start=True, stop=True),
                                        reads=[sel.r, Uacc.r], writes=[pL.r])
                                    S.op("dve", I("tensor_tensor", out=o_bT[64 * e_:64 * e_ + 64, pp, blo:bhi], in0=Uacc[0:64, e_, blo:bhi],
                                        in1=pL[0:64, 0:N], op=ALU.mult), reads=[Uacc.r, pL.r], writes=[o_bT.r])

                        tasks = [(pp_, g_) for pp_ in range(2) for g_ in range(3)]
                        gens = [pair_task(pp_, g_) for (pp_, g_) in tasks]
                        S.op("dve", I("memset", Uacc[:], 0.0), writes=[Uacc.r])
                        next(gens[0])
                        for ti_ in range(len(tasks)):
                            next(gens[ti_])
                            if ti_ + 1 < len(tasks):
                                next(gens[ti_ + 1])
                            for _ in gens[ti_]:
                                pass
                            if tasks[ti_][1] == 2:
                                normalize_pp(tasks[ti_][0])
                                if ti_ + 1 < len(tasks):
                                    S.op("dve", I("memset", Uacc[:], 0.0), writes=[Uacc.r])
                        S.run_block()

                h2T = mk(ust, "h2T", [128, KC, NQM], BF16)
                with ExitStack() as p34:
                    o_aT = mk(p34, "o_aT", [128, 8, NQM], BF16)
                    with ExitStack() as ph:
                        wukv = mk(ph, "wukv", [128, 2, 2048], BF16)
                        wuq = mk(ph, "wuq", [128, 3, 1536], BF16)
                        wuqr = mk(ph, "wuqr", [128, 3, 16, 96], BF16)
                        csq = mk(ph, "csq", [96, 2, NQM], F32)
                        NKT = Sk // 128
                        KT2 = mk(ph, "KT2", [96, SKMAX], BF16)
                        KTs = [KT, KT2]
                        ktn = [S.res("ktn0"), S.res("ktn1")]
                        Vhs = [mk(ph, "Vh%d" % i, [128, SKMAX // 128, 65], BF16) for i in range(2)]
                        QTs = [mk(ph, "QT%d" % i, [96, NQM], BF16) for i in range(2)]
                        t1 = mk(ph, "t1q", [96, 512], F32)
                        t2 = mk(ph, "t2q", [96, 512], F32)
                        NP = 4
                        Pm = [mk(ph, "Pm%d" % i, [128, 2, 512], BF16) for i in range(NP)]
                        Osb = mk(ph, "Osb", [65, 512], F32)
                        pS = [mk(ph, "mS%d" % i, [128, 2, 512], F32, psum=True) for i in range(3)]
                        pO = [mk(ph, "mO%d" % i, [128, 512], F32, psum=True) for i in range(1)]
                        pB = [mk(ph, "mB%d" % i, [128, 512], F32, psum=True) for i in range(1)]
                        pL = pB[0]
                        S.dma("pool", wukv[:], w_ukv.rearrange("(c p) n -> p c n", p=128), wukv.r, writes=[wukv.r])
                        S.dma("pool", wuq[:], w_uq.rearrange("(c p) n -> p c n", p=128), wuq.r, writes=[wuq.r])
                        S.op("dve", I("memset", wuqr[:], 0.0), writes=[wuqr.r])
                        wq4 = w_uq.rearrange("(c p) (h e) -> p c h e", p=128, e=96)
                        for c in range(3):
                            S.dma("pool", wuqr[:, c, :, 64:80], wq4[:, c, :, 80:96], wuqr.r, writes=[wuqr.r])
                            S.dma("pool", wuqr[:, c, :, 80:96], wq4[:, c, :, 64:80], wuqr.r, writes=[wuqr.r])
                        S.op("dve", I("tensor_scalar", out=wuqr[:, :, :, 64:80], in0=wuqr[:, :, :, 64:80], scalar1=-1.0,
                                      scalar2=None, op0=ALU.mult), reads=[wuqr.r], writes=[wuqr.r])
                        S.dma("sp", csq[64:96, :, 0:nq], rope[:, :, qa:qb], csq.r, writes=[csq.r])
                        for b_ in range(2):
                            S.op("dve", I("memset", Vhs[b_][:], 1.0), writes=[Vhs[b_].r])
                        if u is units[0]:
                            for i_ in range(FC):
                                for hf_ in range(2):
                                    S.dma("pool", wup_arr[i_, :, :, 128 * hf_:128 * hf_ + 128],
                                          w_up[:, hf_ * DFF + 128 * i_:hf_ * DFF + 128 * i_ + 128].rearrange("(c p) n -> p c n", p=128),
                                          S.res("cv"))
                            for dst_, src_ in ((wg_bf, w_in[:, GATE0:GATE0 + 2048]), (pa_bf, p_a), (pb_bf, p_b), (wo_bf, w_out),
                                               (wdn_bf, w_down)):
                                rows = dst_.shape[0]
                                for (r0, r1) in split(0, rows, 512):
                                    S.dma("pool", dst_[r0:r1, :], src_[r0:r1, :], S.res("cv"))
                        kt2rope = S.res("kt2rope")
                        S.op("dve", I("tensor_copy", out=KT2[64:96, 0:Sk], in_=KT[64:96, 0:Sk]), writes=[kt2rope])
                        cnt3 = dict(cb=0, cs=0, co=0)
                        qblks = split(0, nq, 512)

                        def build_chunks(h, buf):
                            KTb, Vh, QT = KTs[buf], Vhs[buf], QTs[buf]
                            chunks = []
                            if not u["first"]:
                                def cl_():
                                    S.dma("sp", KTb[0:64, 0:Sk], kscr[h, :, 0:Sk], S.res("kld"), writes=[ktn[buf]])
                                chunks.append(cl_)
                            for kb in range(Sk // 512 if u["first"] else 0):
                                def ck_(kb=kb):
                                    p = pB[0]
                                    cnt3["cb"] += 1
                                    S.op("pe", mm_acc(p[0:64, 0:512], lambda c: wukv[:, c, 128 * h:128 * h + 64],
                                                      lambda c: ckvnT[:, c, kb * 512:kb * 512 + 512], 2),
                                         reads=[wukv.r], writes=[p.r])
                                    S.op("dve", I("tensor_copy", out=KTb[0:64, kb * 512:kb * 512 + 512], in_=p[0:64, 0:512]),
                                         reads=[p.r], writes=[ktn[buf]])
                                chunks.append(ck_)
                            for j0 in range(0, NKT if u["first"] else 0, 8):
                                def cv_(j0=j0):
                                    p = pB[0]
                                    cnt3["cb"] += 1
                                    gsz = min(8, NKT - j0)

                                    def vb(e):
                                        ins = None
                                        for jj in range(gsz):
                                            for c in range(2):
                                                ins = e.matmul(p[:, jj * 64:jj * 64 + 64],
                                                               ckvnT[:, c, (j0 + jj) * 128:(j0 + jj) * 128 + 128],
                                                               wukv[:, c, 128 * h + 64:128 * h + 128], start=(c == 0), stop=(c == 1))
                                        return ins
                                    S.op("pe", vb, reads=[wukv.r], writes=[p.r])
                                    S.op("dve", I("tensor_copy", out=Vh[:, j0:j0 + gsz, 0:64],
                                                  in_=p[:, 0:64 * gsz].rearrange("p (a b) -> p a b", a=gsz)),
                                         reads=[p.r], writes=[Vh.r])
                                chunks.append(cv_)
                            for (blo, bhi) in qblks:
                                def cq_(blo=blo, bhi=bhi):
                                    N = bhi - blo
                                    pq = pB[0]
                                    cnt3["cb"] += 1
                                    pr = pB[0]
                                    S.op("pe", mm_acc(pq[0:96, 0:N], lambda c: wuq[:, c, 96 * h:96 * h + 96],
                                                      lambda c: cqnT[:, c, blo:bhi], 3), reads=[wuq.r], writes=[pq.r])
                                    S.op("dve", I("tensor_copy", out=QT[0:64, blo:bhi], in_=pq[0:64, 0:N]), reads=[pq.r], writes=[QT.r])
                                    S.op("dve", I("tensor_tensor", out=t2[64:96, 0:N], in0=pq[64:96, 0:N], in1=csq[64:96, 0, blo:bhi], op=ALU.mult),
                                         reads=[pq.r, csq.r], writes=[t2.r])
                                    S.op("pe", mm_acc(pr[0:96, 0:N], lambda c: wuqr[:, c, h, :],
                                                      lambda c: cqnT[:, c, blo:bhi], 3), reads=[wuqr.r], writes=[pr.r])
                                    S.op("dve", I("tensor_tensor", out=t1[64:96, 0:N], in0=pr[64:96, 0:N], in1=csq[64:96, 1, blo:bhi], op=ALU.mult),
                                         reads=[pr.r, csq.r], writes=[t1.r])
                                    S.op("dve", I("tensor_tensor", out=QT[64:96, blo:bhi], in0=t1[64:96, 0:N], in1=t2[64:96, 0:N], op=ALU.add),
                                         reads=[t1.r, t2.r], writes=[QT.r])
                                chunks.append(cq_)
                            if not u["first"]:
                                def clv_():
                                    S.dma("sp", Vh[:, 0:NKT, :], vscr[h, :, 0:NKT * 65].rearrange("p (a b) -> p a b", b=65),
                                          S.res("vld"), writes=[Vh.r])
                                chunks.insert(2, clv_)
                            if u["first"] and len(units) > 1:
                                def cs_():
                                    S.dma("pool", kscr[h, :, 0:Sk], KTb[0:64, 0:Sk], S.res("kst"), reads=[ktn[buf]])
                                    S.dma("pool", vscr[h, :, 0:NKT * 65].rearrange("p (a b) -> p a b", b=65), Vh[:, 0:NKT, :],
                                          S.res("vst"), reads=[Vh.r])
                                chunks.append(cs_)
                            return chunks

                        for ch in build_chunks(0, 0):
                            ch()
                        m0 = q0 - qa
                        mblks = split(m0, m0 + (q1 - q0), 512)
                        hcols = ([0] if qa < q0 else []) + ([nq - 1] if qb > q1 else [])
                        nh = len(hcols)
                        hsl = slice(hcols[0], hcols[-1] + 1, (hcols[-1] - hcols[0]) if nh == 2 else 1) if nh else None
                        NG = NKT // 2
                        nsteps = len(mblks) * NG

                        po = pO[0]

                        OsbH = mk(ph, "OsbH", [65, 8], F32)

                        def normA(N, Osb=Osb):
                            S.op("dve", I("tensor_copy", out=Osb[0:65, 0:N], in_=po[0:65, 0:N]), reads=[po.r], writes=[Osb.r])
                            S.op("act", I("activation", out=Osb[64:65, 0:N], in_=Osb[64:65, 0:N], func=AF.Ln), reads=[Osb.r], writes=[Osb.r])
                            S.op("act", I("activation", out=Osb[64:65, 0:N], in_=Osb[64:65, 0:N], func=AF.Exp, scale=-1.0), reads=[Osb.r], writes=[Osb.r])

                        def normB(N, h, osl, Osb=Osb):
                            e_ = h % 2
                            S.op("pe", I("matmul", pL[0:64, 0:N], sel[0:65, 0:64], Osb[0:65, 0:N], start=True, stop=True),
                                 reads=[sel.r, Osb.r], writes=[pL.r])
                            S.op("dve", I("tensor_tensor", out=o_aT[64 * e_:64 * e_ + 64, h // 2, osl], in0=Osb[0:64, 0:N], in1=pL[0:64, 0:N], op=ALU.mult),
                                 reads=[Osb.r, pL.r], writes=[o_aT.r])

                        items = []
                        for h in range(16):
                            for (blo, bhi) in mblks:
                                for g in range(NG):
                                    items.append(("m", h, blo, bhi, g))
                            if nh:
                                items.append(("h", h, 0, 0, 0))
                        per_head = len(mblks) * NG + (1 if nh else 0)
                        state = {}

                        def s1(i):
                            kind, h, blo, bhi, g = items[i]
                            buf = h % 2
                            KTb, QT = KTs[buf], QTs[buf]
                            krd = [ktn[buf], QT.r] + ([kt2rope] if buf == 1 else [])
                            ps_ = pS[cnt3["cs"] % 3]
                            P_ = Pm[cnt3["cs"] % NP]
                            cnt3["cs"] += 1
                            state[i] = P_
                            if kind == "m":
                                N = bhi - blo

                                def f1(e):
                                    ins = None
                                    for j in range(2):
                                        kt = 2 * g + j
                                        ins = e.matmul(ps_[:, j, 0:N], KTb[0:96, kt * 128:kt * 128 + 128], QT[0:96, blo:bhi], start=True, stop=True)
                                    return ins
                                S.op("pe", f1, reads=krd, writes=[ps_.r])
                                S.op("act", I("activation", out=P_[:, :, 0:N], in_=ps_[:, :, 0:N], func=AF.Exp, scale=MLA_SCALE),
                                     reads=[ps_.r], writes=[P_.r])
                            else:
                                def fh(e):
                                    ins = None
                                    for kt in range(NKT):
                                        ins = e.matmul(ps_[:, 0, kt * nh:kt * nh + nh], KTb[0:96, kt * 128:kt * 128 + 128], QT[0:96, hsl], start=True, stop=True)
                                    return ins
                                S.op("pe", fh, reads=krd, writes=[ps_.r])
                                S.op("act", I("activation", out=P_[:, 0, 0:NKT * nh], in_=ps_[:, 0, 0:NKT * nh], func=AF.Exp, scale=MLA_SCALE),
                                     reads=[ps_.r], writes=[P_.r])

                        def s2(i):
                            kind, h, blo, bhi, g = items[i]
                            Vh = Vhs[h % 2]
                            P_ = state.pop(i)
                            if kind == "m":
                                N = bhi - blo

                                def f(e):
                                    ins = None
                                    for j in range(2):
                                        kt = 2 * g + j
                                        ins = e.matmul(po[0:65, 0:N], Vh[:, kt, 0:65], P_[:, j, 0:N], start=(kt == 0), stop=(kt == NKT - 1))
                                    return ins
                                S.op("pe", f, reads=[Vh.r, P_.r], writes=[po.r])
                                if g == NG - 1:
                                    normA(N)
                            else:
                                def fh2(e):
                                    ins = None
                                    for kt in range(NKT):
                                        ins = e.matmul(po[0:65, 0:nh], Vh[:, kt, 0:65], P_[:, 0, kt * nh:kt * nh + nh], start=(kt == 0), stop=(kt == NKT - 1))
                                    return ins
                                S.op("pe", fh2, reads=[Vh.r, P_.r], writes=[po.r])
                                normA(nh, OsbH)

                        def s3(i):
                            kind, h, blo, bhi, g = items[i]
                            if kind == "m":
                                if g == NG - 1:
                                    normB(bhi - blo, h, slice(blo, bhi))
                            else:
                                normB(nh, h, hsl, OsbH)

                        nxt = []
                        LA1, LA2 = 2, 3
                        n_it = len(items)
                        for i in range(n_it + LA1 + LA2):
                            if i < n_it:
                                h = items[i][1]
                                if i % per_head == 0 and h < 15:
                                    while nxt:
                                        nxt.pop(0)()
                                    nxt = build_chunks(h + 1, 1 - (h % 2))
                                    every = max(1, (per_head - 6) // max(1, len(nxt)))
                                s1(i)
                                if nxt and (i % per_head) % every == 0:
                                    nxt.pop(0)()
                            if 0 <= i - LA1 < n_it:
                                s2(i - LA1)
                            if 0 <= i - LA1 - LA2 < n_it:
                                s3(i - LA1 - LA2)
                        while nxt:
                            nxt.pop(0)()
                        S.run_block()

                    with ExitStack() as ph:
                        st = dict(xi=0, pti=0)
                        st["xs"] = [mk(ph, "xs%d" % i, [128, D], F32) for i in range(3)]
                        st["hn"] = [mk(ph, "hn%d" % i, [128, D], F32) for i in range(2)]
                        st["hb"] = [mk(ph, "hb%d" % i, [128, D], BF16) for i in range(2)]
                        st["ss"] = [mk(ph, "ss%d" % i, [128, 1], F32) for i in range(2)]
                        st["sd"] = [mk(ph, "sd%d" % i, [128, 1], F32) for i in range(2)]
                        st["rs"] = [mk(ph, "rs%d" % i, [128, 1], F32) for i in range(2)]
                        st["pt"] = [mk(ph, "pt%d" % i, [128, KC, 128], BF16, psum=True) for i in range(2)]
                        pG = [mk(ph, "pG%d" % i, [128, 512], F32, psum=True) for i in range(4)]
                        pY = [mk(ph, "pY%d" % i, [128, 512], F32, psum=True) for i in range(2)]
                        ws = [mk(ph, "ws%d" % i, [128, KC, 1024], BF16) for i in range(2)]
                        hblk = mk(ph, "hblk", [128, KC, 344], BF16)
                        gaT = mk(ph, "gaT", [128, KC, 344], BF16)
                        gbT = mk(ph, "gbT", [128, KC, 344], BF16)
                        tmpa = mk(ph, "tmpa", [128, 344], F32)
                        tmpb = mk(ph, "tmpb", [128, 344], F32)
                        mrgb = mk(ph, "mrgb", [128, KC, 344], BF16)
                        x1t = [mk(ph, "x1t%d" % i, [128, D], F32) for i in range(1)]
                        cw = 0
                        cg = 0
                        cx = 0
                        wg3 = wg_bf.rearrange("(c p) n -> p c n", p=128)

                        def ld(dram3):
                            nonlocal cw
                            w = ws[cw % 2]
                            cw += 1
                            kc = dram3.shape[1]
                            S.dma("sp", w[:, 0:kc, :], dram3, w.r, writes=[w.r])
                            return w

                        for (blo, bhi) in split(qa, qb, 344):
                            N = bhi - blo
                            tl = split(blo, bhi, 128)
                            xs_of = {}
                            prevc = None
                            for (lo, hi) in tl:
                                nt = hi - lo
                                xs = st["xs"][st["xi"] % 3]
                                st["xi"] += 1
                                xs_of[lo] = xs
                                S.dma("sp", xs[0:nt, :], x[u["base"] + lo:u["base"] + hi, :], xs.r, writes=[xs.r])
                                c_ = emit_normA(st, xs, nt)
                                if prevc is not None:
                                    emit_normB(st, prevc[0], 1, 0, prevc[1], hblk.r)
                                prevc = (c_, hblk[:, :, lo - blo:hi - blo])
                            emit_normB(st, prevc[0], 1, 0, prevc[1], hblk.r)
                            for gi, (gT, gc0) in enumerate(((gaT, GATE0), (gbT, GATE0 + 1024))):
                                w = ld(wg3[:, :, gc0 - GATE0:gc0 - GATE0 + 1024])
                                for m in range(KC):
                                    p = pG[cg % 4]
                                    cg += 1
                                    S.op("pe", mm_acc(p[:, 0:N], lambda c, w=w, m=m: w[:, c, 128 * m:128 * m + 128],
                                                      lambda c: hblk[:, c, 0:N], KC), reads=[w.r, hblk.r], writes=[p.r])
                                    S.op("act", I("activation", out=gT[:, m, 0:N], in_=p[:, 0:N], func=AF.Sigmoid),
                                         reads=[p.r], writes=[gT.r])
                            wa = ld(pa_bf.rearrange("(c p) n -> p c n", p=128))
                            wb_ = ld(pb_bf.rearrange("(c p) n -> p c n", p=128))
                            for m in range(KC):
                                p = pG[cg % 4]
                                cg += 1
                                p2 = pG[cg % 4]
                                cg += 1
                                S.op("pe", mm_acc(p[:, 0:N], lambda c, m=m: wa[:, c, 128 * m:128 * m + 128],
                                                  lambda c: o_aT[:, c, blo - qa:bhi - qa], 8), reads=[wa.r], writes=[p.r])
                                S.op("pe", mm_acc(p2[:, 0:N], lambda c, m=m: wb_[:, c, 128 * m:128 * m + 128],
                                                  lambda c: o_bT[:, c, blo - qa:bhi - qa], 2), reads=[wb_.r], writes=[p2.r])
                                S.op("dve", I("tensor_tensor", out=tmpa[:, 0:N], in0=p[:, 0:N], in1=gaT[:, m, 0:N], op=ALU.mult),
                                     reads=[p.r, gaT.r], writes=[tmpa.r])
                                S.op("dve", I("tensor_tensor", out=tmpb[:, 0:N], in0=p2[:, 0:N], in1=gbT[:, m, 0:N], op=ALU.mult),
                                     reads=[p2.r, gbT.r], writes=[tmpb.r])
                                S.op("dve", I("tensor_tensor", out=mrgb[:, m, 0:N], in0=tmpa[:, 0:N], in1=tmpb[:, 0:N], op=ALU.add),
                                     reads=[tmpa.r, tmpb.r], writes=[mrgb.r])
                            w = ld(wo_bf.rearrange("(c p) n -> p c n", p=128))
                            for (lo, hi) in tl:
                                nt = hi - lo
                                xs = xs_of[lo]
                                x1 = x1t[0]
                                cx += 1
                                for nh in range(2):
                                    p = pY[nh]
                                    S.op("pe", mm_acc(p[0:nt, :], lambda c, lo=lo, hi=hi: mrgb[:, c, lo - blo:hi - blo],
                                                      lambda c, w=w, nh=nh: w[:, c, 512 * nh:512 * nh + 512], KC),
                                         reads=[w.r, mrgb.r], writes=[p.r])
                                    S.op("dve", I("tensor_tensor", out=x1[0:nt, 512 * nh:512 * nh + 512], in0=p[0:nt, :], in1=modbc[0:nt, 2, 512 * nh:512 * nh + 512], op=ALU.mult),
                                        reads=[p.r, modbc.r], writes=[x1.r])
                                S.op("dve", I("tensor_tensor", out=x1[0:nt, :], in0=x1[0:nt, :], in1=xs[0:nt, :], op=ALU.add),
                                     reads=[x1.r, xs.r], writes=[x1.r])
                                S.dma("pool", x1s[lo - qa:hi - qa, :], x1[0:nt, :], x1.r, reads=[x1.r])
                                emit_norm(st, x1, nt, 4, 3, h2T[:, :, lo - qa:hi - qa], h2T.r)
                        S.run_block()

                with ExitStack() as ph:
                    wdn = mk(ph, "wdn", [128, FC, D], BF16)
                    actT = mk(ph, "actT", [128, FC, 512], BF16)
                    wup = [mk(ph, "wup%d" % i, [128, KC, 256], BF16) for i in range(2)]
                    u_sb = mk(ph, "u_sb", [128, 516], F32)
                    acc = mk(ph, "acc", [128, 512], F32)
                    gl = mk(ph, "gl", [128, 512], F32)
                    x1l = [mk(ph, "x1l%d" % i, [128, D], F32) for i in range(2)]
                    x2 = mk(ph, "x2", [128, D], F32)
                    sq = mk(ph, "sq5", [128, D], F32)
                    ot = mk(ph, "ot", [128, D], F32)
                    ss = mk(ph, "ss5", [128, 1], F32)
                    sd = mk(ph, "sd5", [128, 1], F32)
                    rs = mk(ph, "rs5", [128, 1], F32)
                    pU = [mk(ph, "pU%d" % i, [128, 512], F32, psum=True) for i in range(4)]
                    pV = [mk(ph, "pV%d" % i, [128, 512], F32, psum=True) for i in range(2)]
                    pD = [mk(ph, "pD%d" % i, [128, 512], F32, psum=True) for i in range(2)]
                    S.dma("sp", wdn[:], wdn_bf.rearrange("(c p) n -> p c n", p=128), wdn.r, writes=[wdn.r])
                    cu = 0
                    cwu = 0
                    cxl = 0
                    for (o_lo, o_hi) in split(q0, q1, 512):
                        no = o_hi - o_lo
                        c_lo = max(qa, o_lo - 1)
                        c_hi = min(qb, o_hi + 1)
                        ncol = c_hi - c_lo
                        j0 = o_lo - c_lo
                        pieces = split(0, ncol, (ncol + 1) // 2)
                        for i in range(FC):
                            w = wup[cwu % 2]
                            cwu += 1
                            S.dma("sp", w[:, :, :], wup_arr[i], w.r, writes=[w.r])
                            pus = []
                            for pi, (a_, b_) in enumerate(pieces):
                                pu = pU[cu % 4]
                                cu += 1
                                pus.append(pu)
                                S.op("pe", mm_acc(pu[:, 0:b_ - a_], lambda c, w=w: w[:, c, 0:128],
                                                  lambda c, a_=a_, b_=b_: h2T[:, c, c_lo - qa + a_:c_lo - qa + b_], KC),
                                     reads=[w.r], writes=[pu.r])
                                S.op("act", I("activation", out=u_sb[:, a_:b_], in_=pu[:, 0:b_ - a_], func=AF.Copy),
                                     reads=[pu.r], writes=[u_sb.r])
                            for pi, (a_, b_) in enumerate(pieces):
                                pv_ = pV[pi]
                                S.op("pe", mm_acc(pv_[:, 0:b_ - a_], lambda c, w=w: w[:, c, 128:256],
                                                  lambda c, a_=a_, b_=b_: h2T[:, c, c_lo - qa + a_:c_lo - qa + b_], KC),
                                     reads=[w.r], writes=[pv_.r])
                            S.op("dve", I("tensor_scalar", out=acc[:, 0:no], in0=u_sb[:, j0:j0 + no], scalar1=convw[:, i, 1:2],
                                                                     scalar2=convb[:, i:i + 1], op0=ALU.mult, op1=ALU.add),
                                 reads=[u_sb.r, convw.r, convb.r], writes=[acc.r])
                            if j0 == 1:
                                la, lb, ls = 0, no, 0
                            else:
                                la, lb, ls = 1, no, 0
                            S.op("dve", I("scalar_tensor_tensor", out=acc[:, la:lb], in0=u_sb[:, ls:ls + (lb - la)], scalar=convw[:, i, 0:1], in1=acc[:, la:lb],
                                op0=ALU.mult, op1=ALU.add), reads=[u_sb.r, convw.r, acc.r], writes=[acc.r])
                            if c_hi > o_hi:
                                ra_, rb_ = 0, no
                            else:
                                ra_, rb_ = 0, no - 1
                            S.op("dve", I("scalar_tensor_tensor", out=acc[:, ra_:rb_], in0=u_sb[:, j0 + 1 + ra_:j0 + 1 + rb_], scalar=convw[:, i, 2:3], in1=acc[:, ra_:rb_],
                                op0=ALU.mult, op1=ALU.add), reads=[u_sb.r, convw.r, acc.r], writes=[acc.r])
                            S.op("act", I("activation", out=gl[:, 0:no], in_=acc[:, 0:no], func=AF.Gelu_apprx_tanh),
                                 reads=[acc.r], writes=[gl.r])
                            for pi, (a_, b_) in enumerate(pieces):
                                lo_ = max(a_, j0)
                                hi_ = min(b_, j0 + no)
                                if hi_ <= lo_:
                                    continue
                                pv_ = pV[pi]
                                S.op("dve", I("tensor_tensor", out=actT[:, i, lo_ - j0:hi_ - j0], in0=gl[:, lo_ - j0:hi_ - j0], in1=pv_[:, lo_ - a_:hi_ - a_], op=ALU.mult),
                                    reads=[gl.r, pv_.r], writes=[actT.r])
                        for (lo, hi) in split(o_lo, o_hi, 128):
                            nt = hi - lo
                            xl = x1l[cxl % 2]
                            cxl += 1
                            S.dma("sp", xl[0:nt, :], x1s[lo - qa:hi - qa, :], xl.r, writes=[xl.r])
                            for nh in range(2):
                                p = pD[nh]
                                S.op("pe", mm_acc(p[0:nt, :], lambda c, lo=lo, hi=hi: actT[:, c, lo - o_lo:hi - o_lo],
                                                  lambda c, nh=nh: wdn[:, c, 512 * nh:512 * nh + 512], FC),
                                     reads=[wdn.r, actT.r], writes=[p.r])
                                S.op("dve", I("tensor_tensor", out=x2[0:nt, 512 * nh:512 * nh + 512], in0=p[0:nt, :], in1=modbc[0:nt, 5, 512 * nh:512 * nh + 512], op=ALU.mult),
                                    reads=[p.r, modbc.r], writes=[x2.r])
                            S.op("dve", I("tensor_tensor", out=x2[0:nt, :], in0=x2[0:nt, :], in1=xl[0:nt, :], op=ALU.add),
                                 reads=[x2.r, xl.r], writes=[x2.r])
                            S.op("act", I("activation", out=sq[0:nt, :], in_=x2[0:nt, :], func=AF.Square, accum_out=ss[0:nt, 0:1]),
                                 reads=[x2.r], writes=[sq.r, ss.r])
                            S.op("act", I("activation", out=sd[0:nt, 0:1], in_=ss[0:nt, 0:1], func=AF.Ln,
                                          bias=eps_t[0:nt, 0:1], scale=1.0 / D), reads=[ss.r, eps_t.r], writes=[sd.r])
                            S.op("act", I("activation", out=rs[0:nt, 0:1], in_=sd[0:nt, 0:1], func=AF.Exp, scale=-0.5), reads=[sd.r], writes=[rs.r])
                            S.op("dve", I("scalar_tensor_tensor", out=ot[0:nt, :], in0=x2[0:nt, :], scalar=rs[0:nt, 0:1],
                                                                                in1=gf_bc[0:nt, :], op0=ALU.mult, op1=ALU.mult),
                                 reads=[x2.r, rs.r, gf_bc.r], writes=[ot.r])
                            yr = u["ybase"] + (lo - q0)
                            S.dma("pool", y[yr:yr + nt, :], ot[0:nt, :], ot.r, reads=[ot.r])
                    S.run_block()
    return nc


FULL_CFG = dict(SA=2048, SB=8192, QB=4096, UQ=1024)


def rope_table(pos):
    inv = (np.float32(10000.0) ** (-(np.arange(0, 32, 2, dtype=np.float32)) / np.float32(32))).astype(np.float32)
    ang = (pos.astype(np.float32)[:, None] * inv[None, :]).astype(np.float32)
    c = np.cos(ang).astype(np.float32).T
    s = np.sin(ang).astype(np.float32).T
    tab = np.stack([np.concatenate([c, c], 0), np.concatenate([s, s], 0)], axis=1)
    return np.ascontiguousarray(tab.astype(np.float32))


def const_tables():
    kk = np.arange(128)[:, None]
    cc = np.arange(256)[None, :]
    rel = kk + 64 - cc
    arel = np.abs(rel).astype(np.float32)
    mband = np.where((cc >= kk) & (cc <= kk + 128), 0.0, NEG).astype(np.float32)
    sel = np.zeros((128, 64), np.float32)
    sel[64, :] = 1.0
    return np.eye(128, dtype=np.float32), sel, arel, mband


def core_inputs(cfg, xa, xb, ca, cb, W, reverse_b):
    SB = cfg["SB"]
    f = lambda a: np.ascontiguousarray(np.asarray(a, dtype=np.float32))
    if reverse_b:
        xb = xb[::-1]
        posb = (SB - 1 - np.arange(SB))
        cwb = W["conv_w"][::-1]
    else:
        posb = np.arange(SB)
        cwb = W["conv_w"]
    ident, sel, arel, mband = const_tables()
    pl = lambda v, k: f(np.asarray(v).reshape(k, 128).T)
    cwl = lambda cw: np.asarray(cw).reshape(3, FC, 128).transpose(2, 1, 0)
    m = {
        "x": f(np.concatenate([xa, xb], 0)),
        "cT": f(np.stack([pl(ca, KC), pl(cb, KC)], 0)),
        "ada_w": f(W["ada_w"]), "ada_b": f(W["ada_b"]), "norm1_g": f(W["norm1_g"]), "w_in": f(W["w_in"]),
        "gq": pl(W["q_norm_g"], 3), "gkv": pl(W["kv_norm_g"], 2), "w_uq": f(W["w_uq"]), "w_ukv": f(W["w_ukv"]),
        "p_a": f(W["p_a"]), "p_b": f(W["p_b"]), "w_out": f(W["w_out"]), "norm2_g": f(W["norm2_g"]),
        "w_up": f(W["w_up"]), "convw": f(np.stack([cwl(W["conv_w"]), cwl(cwb)], 0)), "convb": pl(W["conv_b"], FC),
        "w_down": f(W["w_down"]), "normf_g": f(W["normf_g"]),
        "ropeA": rope_table(np.arange(cfg["SA"])), "ropeB": rope_table(posb),
        "ident": ident, "sel": sel, "arel": arel, "mband": mband,
    }
    return m


_NC_CACHE = {}


def kernel(x_prompt, x_sample, c_prompt, c_sample, ada_w, ada_b, norm1_g, w_in, q_norm_g, kv_norm_g,
           w_uq, w_ukv, p_a, p_b, w_out, norm2_g, w_up, conv_w, conv_b, w_down, normf_g):
    cfg = FULL_CFG
    W = dict(ada_w=np.asarray(ada_w)[0], ada_b=np.asarray(ada_b)[0], norm1_g=np.asarray(norm1_g)[0], w_in=np.asarray(w_in)[0],
             q_norm_g=np.asarray(q_norm_g)[0], kv_norm_g=np.asarray(kv_norm_g)[0], w_uq=np.asarray(w_uq)[0],
             w_ukv=np.asarray(w_ukv)[0], p_a=np.asarray(p_a)[0], p_b=np.asarray(p_b)[0], w_out=np.asarray(w_out)[0],
             norm2_g=np.asarray(norm2_g)[0], w_up=np.asarray(w_up)[0], conv_w=np.asarray(conv_w)[0],
             conv_b=np.asarray(conv_b)[0], w_down=np.asarray(w_down)[0], normf_g=np.asarray(normf_g))
    x_prompt = np.asarray(x_prompt)
    x_sample = np.asarray(x_sample)
    c_prompt = np.asarray(c_prompt)
    c_sample = np.asarray(c_sample)
    in_maps = []
    for i in range(8):
        in_maps.append(core_inputs(cfg, x_prompt[i], x_sample[i // 2], c_prompt[i], c_sample[i // 2], W, reverse_b=(i % 2 == 1)))
    nc = build(cfg)
    res = run_bass_kernel_spmd(nc, in_maps, core_ids=list(range(8)))
    SA, QB = cfg["SA"], cfg["QB"]
    y_prompt = np.empty((8, SA, D), np.float32)
    y_sample = np.empty((4, cfg["SB"], D), np.float32)
    for i in range(8):
        yy = np.asarray(res.results[i]["y"])
        y_prompt[i] = yy[0:SA]
        if i % 2 == 0:
            y_sample[i // 2, 0:QB] = yy[SA:SA + QB]
        else:
            y_sample[i // 2, cfg["SB"] - QB:] = yy[SA:SA + QB][::-1]
    return (y_prompt, y_sample)
```
